# Optimizing a Trainium2 kernel written in Bass

```python
import jax, jax.numpy as jnp
from jax import lax
import numpy as np

D_MODEL = 1024
BATCH = 4
SEQ = 8192
DEPTH = 1

RNN_WIDTH = D_MODEL
RNN_BLOCKS = 8
RNN_BLOCK_W = RNN_WIDTH // RNN_BLOCKS
CONV_WIDTH = 4
LRU_C = 8.0
MLA_HEADS = 8
QK_NOPE = 128
QK_ROPE = 64
V_HEAD = D_MODEL // MLA_HEADS
Q_LORA = D_MODEL // 4
KV_LORA = D_MODEL // 4
ROPE_THETA = 10000.0
Q_BLOCK = 128
D_FF = 4 * D_MODEL
EPS = 1e-6

IN_SIZES = (RNN_WIDTH, RNN_WIDTH, Q_LORA, KV_LORA, QK_ROPE, D_MODEL, D_MODEL)
IN_TOTAL = sum(IN_SIZES)

kernel_name = "hybrid_rglru_mla_sqrelu"


def rmsnorm(x, g):
    xf = x.astype(jnp.float32)
    y = xf * lax.rsqrt(jnp.mean(xf * xf, axis=-1, keepdims=True) + EPS) * g.astype(jnp.float32)
    return y.astype(x.dtype)


def split_cols(z, sizes):
    offs = np.cumsum(sizes)[:-1].tolist()
    return jnp.split(z, offs, axis=-1)


def causal_depthwise_conv(x, w, b):
    c = x.shape[-1]
    y = lax.conv_general_dilated(
        x, w[:, None, :].astype(x.dtype), window_strides=(1,), padding=[(CONV_WIDTH - 1, 0)],
        dimension_numbers=("NWC", "WIO", "NWC"), feature_group_count=c)
    return y + b


def block_diag_linear(x, w, b):
    bsz, s, c = x.shape
    xb = x.reshape(bsz, s, RNN_BLOCKS, RNN_BLOCK_W)
    y = jnp.einsum("bsnc,ncd->bsnd", xb, w) + b
    return y.reshape(bsz, s, c)


def rg_lru(xa, wa, ba, wx, bx, lam):
    xf = xa.astype(jnp.float32)
    r = jax.nn.sigmoid(block_diag_linear(xa, wa, ba).astype(jnp.float32))
    i = jax.nn.sigmoid(block_diag_linear(xa, wx, bx).astype(jnp.float32))
    log_a = -LRU_C * r * jax.nn.softplus(-lam.astype(jnp.float32))
    a = jnp.exp(log_a)
    b = jnp.sqrt(-jnp.expm1(2.0 * log_a)) * (i * xf)

    def combine(e1, e2):
        a1, b1 = e1
        a2, b2 = e2
        return a1 * a2, a2 * b1 + b2

    _, h = lax.associative_scan(combine, (a, b), axis=1)
    return h


def rope_tables(seq):
    pos = jnp.arange(seq, dtype=jnp.float32)
    inv_freq = 1.0 / (ROPE_THETA ** (jnp.arange(0, QK_ROPE, 2, dtype=jnp.float32) / QK_ROPE))
    ang = pos[:, None] * inv_freq[None, :]
    cos = jnp.cos(ang)
    sin = jnp.sin(ang)
    return jnp.concatenate([cos, cos], -1), jnp.concatenate([sin, sin], -1)


def apply_rope(x, cos, sin):
    half = x.shape[-1] // 2
    rot = jnp.concatenate([-x[..., half:], x[..., :half]], axis=-1)
    return (x.astype(jnp.float32) * cos + rot.astype(jnp.float32) * sin).astype(x.dtype)


def mla(c_q, c_kv, k_rope, q_norm, w_uq, kv_norm, w_ukv):
    bsz, s, _ = c_q.shape
    q = (rmsnorm(c_q, q_norm) @ w_uq).reshape(bsz, s, MLA_HEADS, QK_NOPE + QK_ROPE)
    kv = (rmsnorm(c_kv, kv_norm) @ w_ukv).reshape(bsz, s, MLA_HEADS, QK_NOPE + V_HEAD)
    q_nope, q_rope = q[..., :QK_NOPE], q[..., QK_NOPE:]
    k_nope, v = kv[..., :QK_NOPE], kv[..., QK_NOPE:]
    cos, sin = rope_tables(s)
    q_rope = apply_rope(q_rope, cos[None, :, None, :], sin[None, :, None, :])
    k_rope = apply_rope(k_rope, cos[None], sin[None])
    scale = (QK_NOPE + QK_ROPE) ** -0.5
    nb = s // Q_BLOCK
    qn_b = q_nope.reshape(bsz, nb, Q_BLOCK, MLA_HEADS, QK_NOPE).transpose(1, 0, 2, 3, 4)
    qr_b = q_rope.reshape(bsz, nb, Q_BLOCK, MLA_HEADS, QK_ROPE).transpose(1, 0, 2, 3, 4)
    kpos = jnp.arange(s)
    qpos_b = kpos.reshape(nb, Q_BLOCK)
    neg = jnp.finfo(jnp.float32).min

    def one_block(args):
        qn, qr, qpos = args
        sc = jnp.einsum("bqhd,bkhd->bhqk", qn, k_nope, preferred_element_type=jnp.float32)
        sc = sc + jnp.einsum("bqhr,bkr->bhqk", qr, k_rope, preferred_element_type=jnp.float32)
        mask = qpos[:, None] >= kpos[None, :]
        sc = jnp.where(mask[None, None], sc * scale, neg)
        p = jax.nn.softmax(sc, axis=-1)
        return jnp.einsum("bhqk,bkhd->bqhd", p.astype(v.dtype), v)

    o = lax.map(one_block, (qn_b, qr_b, qpos_b))
    return o.transpose(1, 0, 2, 3, 4).reshape(bsz, s, MLA_HEADS * V_HEAD)


def setup_inputs(seed: int = 0) -> dict:
    key = jax.random.key(seed)
    ks = jax.random.split(key, 20)
    f32 = jnp.float32
    nrm = lambda k, shape, fan: jax.random.normal(k, shape, f32) * fan ** -0.5
    gain = lambda k, n: 1.0 + 0.02 * jax.random.normal(k, (n,), f32)
    u = jax.random.uniform(ks[9], (RNN_WIDTH,), f32, 0.9, 0.999)
    a0 = u ** (1.0 / LRU_C)
    lru_lambda = jnp.log(a0) - jnp.log1p(-a0)
    return {
        "x": jax.random.normal(ks[0], (BATCH, SEQ, D_MODEL), f32),
        "norm_mix": gain(ks[1], D_MODEL),
        "w_in": nrm(ks[2], (D_MODEL, IN_TOTAL), D_MODEL),
        "conv_w": nrm(ks[3], (CONV_WIDTH, RNN_WIDTH), CONV_WIDTH),
        "conv_b": 0.01 * jax.random.normal(ks[4], (RNN_WIDTH,), f32),
        "lru_wa": nrm(ks[5], (RNN_BLOCKS, RNN_BLOCK_W, RNN_BLOCK_W), RNN_BLOCK_W),
        "lru_ba": 0.01 * jax.random.normal(ks[6], (RNN_BLOCKS, RNN_BLOCK_W), f32),
        "lru_wx": nrm(ks[7], (RNN_BLOCKS, RNN_BLOCK_W, RNN_BLOCK_W), RNN_BLOCK_W),
        "lru_bx": 0.01 * jax.random.normal(ks[8], (RNN_BLOCKS, RNN_BLOCK_W), f32),
        "lru_lambda": lru_lambda,
        "q_norm": gain(ks[10], Q_LORA),
        "w_uq": nrm(ks[11], (Q_LORA, MLA_HEADS * (QK_NOPE + QK_ROPE)), Q_LORA),
        "kv_norm": gain(ks[12], KV_LORA),
        "w_ukv": nrm(ks[13], (KV_LORA, MLA_HEADS * (QK_NOPE + V_HEAD)), KV_LORA),
        "w_out": nrm(ks[14], (D_MODEL, D_MODEL), D_MODEL),
        "norm_mlp": gain(ks[15], D_MODEL),
        "w_up": nrm(ks[16], (D_MODEL, D_FF), D_MODEL),
        "w_down": nrm(ks[17], (D_FF, D_MODEL), D_FF),
        "norm_final": gain(ks[18], D_MODEL),
    }


def reference(x, norm_mix, w_in, conv_w, conv_b, lru_wa, lru_ba, lru_wx, lru_bx, lru_lambda,
              q_norm, w_uq, kv_norm, w_ukv, w_out, norm_mlp, w_up, w_down, norm_final):
    h = x
    for _ in range(DEPTH):
        z = rmsnorm(h, norm_mix) @ w_in
        rnn_x, rnn_gate, c_q, c_kv, k_rope, gate_a, gate_b = split_cols(z, IN_SIZES)
        xa = causal_depthwise_conv(rnn_x, conv_w, conv_b)
        hr = rg_lru(xa, lru_wa, lru_ba, lru_wx, lru_bx, lru_lambda)
        y_a = hr * jax.nn.gelu(rnn_gate.astype(jnp.float32))
        y_b = mla(c_q, c_kv, k_rope, q_norm, w_uq, kv_norm, w_ukv).astype(jnp.float32)
        merged = (jax.nn.sigmoid(gate_a.astype(jnp.float32)) * y_a
                  + jax.nn.sigmoid(gate_b.astype(jnp.float32)) * y_b).astype(h.dtype)
        h = h + merged @ w_out
        u = rmsnorm(h, norm_mlp) @ w_up
        h = h + jnp.square(jax.nn.relu(u)) @ w_down
    return rmsnorm(h, norm_final)
```

```python
import numpy as np
from contextlib import ExitStack
import concourse.bass as bass
import concourse.mybir as mybir
from concourse.bass_utils import run_bass_kernel_spmd

F32 = mybir.dt.float32
BF16 = mybir.dt.bfloat16
AF = mybir.ActivationFunctionType
ALU = mybir.AluOpType

ENGS = ["tensor", "vector", "scalar", "gpsimd", "sync"]
SAME_ENGINE_SYNC = True

SEQ = 8192
OWN = 4096
D = 1024
NT = 16
EPS = 1e-6
WIN_COLS = 4736
SBUF_BASE = 16640
SBUF_LIMIT = 229300


class Op:
    __slots__ = ("eng", "fn", "deps", "dma_key", "signal", "sigval", "idx", "waits", "small")


class Sched:
    def __init__(self):
        self.ops = {e: [] for e in ENGS}
        self.res = {}
        self.dma_count = {}
        self.pending_barrier = {}
        self.all_dma_last = {}
        self.force_small = False

    def op(self, eng, fn, reads=(), writes=(), dma_key=None, small=False):
        o = Op()
        o.small = small or self.force_small
        o.eng = eng
        o.fn = fn
        o.dma_key = dma_key
        o.signal = False
        o.sigval = None
        o.waits = []
        deps = set()
        for k in reads:
            st = self.res.get(k)
            if st is not None and st[0] is not None:
                deps.add(st[0])
        for k in writes:
            st = self.res.get(k)
            if st is not None:
                if st[0] is not None:
                    deps.add(st[0])
                last = {}
                for r in st[1]:
                    if r.dma_key is not None:
                        deps.add(r)
                    else:
                        p = last.get(r.eng)
                        if p is None or p.idx < r.idx:
                            last[r.eng] = r
                deps.update(last.values())
        for k in reads:
            st = self.res.get(k)
            if st is None:
                st = [None, []]
                self.res[k] = st
            st[1].append(o)
        for k in writes:
            self.res[k] = [o, []]
        pb = self.pending_barrier.pop(eng, None)
        if pb:
            deps.update(pb)
        deps.discard(o)
        o.deps = deps
        o.idx = len(self.ops[eng])
        self.ops[eng].append(o)
        if dma_key is not None:
            c = self.dma_count.get(dma_key, 0) + 1
            self.dma_count[dma_key] = c
            o.sigval = 16 * c
            o.signal = True
            self.all_dma_last[dma_key] = o
        return o

    def barrier(self):
        deps = set()
        for e in ENGS:
            for o in reversed(self.ops[e]):
                if o.dma_key is None:
                    deps.add(o)
                    break
        deps.update(self.all_dma_last.values())
        for e in ENGS:
            cur = set(self.pending_barrier.get(e, set())) | deps
            self.pending_barrier[e] = cur

    def finalize(self, nc, stack):
        for e in ENGS:
            for o in self.ops[e]:
                for d in o.deps:
                    if d.dma_key is None:
                        if d.eng == e and (e == "tensor" or not (SAME_ENGINE_SYNC or d.small or o.small)):
                            continue
                        d.signal = True
        sems = {}
        for e in ENGS:
            sems[("eng", e)] = stack.enter_context(nc.semaphore("s_" + e))
        for i, k in enumerate(self.dma_count):
            sems[("dma", k)] = stack.enter_context(nc.semaphore("d%d" % i))
        for e in ENGS:
            c = 0
            for o in self.ops[e]:
                if o.dma_key is None and o.signal:
                    c += 1
                    o.sigval = c
        for e in ENGS:
            known = {}
            for o in self.ops[e]:
                waits = {}
                for d in o.deps:
                    if d.dma_key is None:
                        if d.eng == e and (e == "tensor" or not (SAME_ENGINE_SYNC or d.small or o.small)):
                            continue
                        key = ("eng", d.eng)
                    else:
                        key = ("dma", d.dma_key)
                    v = d.sigval
                    if known.get(key, 0) >= v:
                        continue
                    if waits.get(key, 0) < v:
                        waits[key] = v
                for key, v in waits.items():
                    known[key] = v
                o.waits = [(sems[key], v) for key, v in waits.items()]
        self.sems = sems

    def emit(self, block):
        sems = self.sems

        def run(e):
            def body(eng):
                for o in self.ops[e]:
                    for (s, v) in o.waits:
                        eng.wait_ge(s, v)
                    ins = o.fn(eng)
                    if o.signal:
                        if o.dma_key is not None:
                            ins.then_inc(sems[("dma", o.dma_key)], 16)
                        else:
                            ins.then_inc(sems[("eng", e)], 1)
            return body

        for e in ENGS:
            if self.ops[e]:
                getattr(block, e)(run(e))


class SB:
    def __init__(self, nc):
        self.nc = nc
        self.top = SBUF_BASE
        self.n = 0
        self.peak = 0

    def alloc(self, name, shape, dtype):
        es = 2 if dtype == BF16 else 4
        size = es
        for s in shape[1:]:
            size *= s
        size = (size + 63) // 64 * 64
        assert self.top + size <= SBUF_LIMIT, (name, self.top, size)
        self.n += 1
        t = self.nc.alloc_sbuf_tensor_at("%s_%d" % (name, self.n), list(shape), dtype, offset=self.top)
        self.top += size
        self.peak = max(self.peak, self.top)
        return t


VC = {}
_o = 0
for _n, _w in [("nmix", 8), ("cw", 32), ("cb", 8), ("ba", 8), ("bx", 8), ("lam", 8), ("qn", 2), ("kvn", 2),
               ("nmlp", 8), ("flag", 1), ("kbias", 1)]:
    VC[_n] = _o
    _o += _w
NV = _o


def build_program(debug=False):
    nc = bass.Bass("TRN2", target_bir_lowering=False)

    def din(name, shape, dt=F32):
        return nc.dram_tensor(name, list(shape), dt, kind="ExternalInput").ap()

    def dscr(name, shape, dt):
        kind = "ExternalOutput" if debug else "Internal"
        return nc.dram_tensor(name, list(shape), dt, kind=kind).ap()

    x_all = din("x_all", [SEQ, D])
    w_in = din("w_in", [D, WIN_COLS])
    w_uq = din("w_uq", [256, 2048])
    w_ukv = din("w_ukv", [256, 2048])
    w_out = din("w_out", [D, D])
    w_up = din("w_up", [D, 4096])
    w_down = din("w_down", [4096, D])
    lru_wa = din("lru_wa", [8, 128, 128])
    lru_wx = din("lru_wx", [8, 128, 128])
    vecs_d = din("vecs", [128, NV])
    nfb_d = din("nfb", [128, D])
    cos_d = din("cosT", [64, SEQ])
    sin_d = din("sinT", [64, SEQ])
    ident_d = din("ident", [128, 128])
    cmask_d = din("cmask", [128, 128])
    kbrow_d = din("kbrow", [1, SEQ])
    out = nc.dram_tensor("out", [OWN, D], F32, kind="ExternalOutput").ap()

    ckvn_d = dscr("ckvn_d", [128, 2, SEQ], BF16)
    krope_d = dscr("krope_d", [64, SEQ], BF16)
    cqn_d = dscr("cqn_d", [128, 2, OWN], BF16)
    ma_d = dscr("ma_d", [128, 8, OWN], BF16)
    thb_d = dscr("thb_d", [128, 8, OWN], BF16)
    mg_d = dscr("mg_d", [128, 8, OWN], BF16)
    h1_d = dscr("h1_d", [OWN, D], F32)
    h1nT_d = dscr("h1nT_d", [128, 8, OWN], BF16)

    S = Sched()
    sb = SB(nc)
    st = ExitStack()
    ps = [st.enter_context(nc.psum_tensor("ps%d" % i, [128, 512], F32)) for i in range(8)]
    PK = ["ps%d" % i for i in range(8)]

    def V(fn, r=(), w=(), small=False):
        return S.op("vector", fn, r, w, small=small)

    def A(fn, r=(), w=(), small=False):
        return S.op("scalar", fn, r, w, small=small)

    def T(fn, r=(), w=()):
        return S.op("tensor", fn, r, w)

    def G(fn, r=(), w=()):
        return S.op("gpsimd", fn, r, w)

    def DQ(fn, r, w, key):
        return S.op("sync", fn, r, w, dma_key=key)

    def GQ(fn, r, w, key):
        return S.op("gpsimd", fn, r, w, dma_key=key)

    vec = sb.alloc("vec", [128, NV], F32)
    identf = sb.alloc("identf", [128, 128], F32)
    ident = sb.alloc("ident", [128, 128], BF16)
    cmaskf = sb.alloc("cmaskf", [128, 128], F32)
    cmask = sb.alloc("cmask", [128, 128], BF16)
    ones = sb.alloc("ones", [128, 128], BF16)
    onesf = sb.alloc("onesf", [128, 128], F32)
    gB1 = sb.alloc("gB1", [128, 8, 128], BF16)
    gB2 = sb.alloc("gB2", [128, 8, 128], BF16)
    c1q = sb.alloc("c1q", [128, 8], F32)
    c1h = sb.alloc("c1h", [128, 8], F32)
    bah = sb.alloc("bah", [128, 8], F32)
    bxh = sb.alloc("bxh", [128, 8], F32)
    tmp8 = [sb.alloc("tmp8_%d" % i, [128, 8], F32) for i in range(4)]
    hst = sb.alloc("hst", [128, 8], F32)
    persist_top = sb.top

    S.force_small = True
    DQ(lambda e: e.dma_start(out=vec[:], in_=vecs_d), [], ["vec"], "vec")
    DQ(lambda e: e.dma_start(out=identf[:], in_=ident_d), [], ["identf"], "identf")
    DQ(lambda e: e.dma_start(out=cmaskf[:], in_=cmask_d), [], ["cmaskf"], "cmaskf")
    V(lambda e: e.tensor_copy(out=ident[:], in_=identf[:]), ["identf"], ["ident"])
    V(lambda e: e.tensor_copy(out=cmask[:], in_=cmaskf[:]), ["cmaskf"], ["cmask"])
    V(lambda e: e.memset(onesf[:], 1.0), [], ["onesf"])
    V(lambda e: e.tensor_copy(out=ones[:], in_=onesf[:]), ["onesf"], ["ones"])
    V(lambda e: e.memset(hst[:], 0.0), [], ["hst"])
    for c in range(8):
        V(lambda e, c=c: e.tensor_scalar(out=gB1[:, c, :], in0=onesf[:], scalar1=vec[:, VC["nmix"] + c:VC["nmix"] + c + 1], scalar2=None, op0=ALU.mult),
          ["onesf", "vec"], ["gB1"])
        V(lambda e, c=c: e.tensor_scalar(out=gB2[:, c, :], in0=onesf[:], scalar1=vec[:, VC["nmlp"] + c:VC["nmlp"] + c + 1], scalar2=None, op0=ALU.mult),
          ["onesf", "vec"], ["gB2"])
    lam = vec[:, VC["lam"]:VC["lam"] + 8]
    e_, w_, l_, d_ = tmp8
    A(lambda e: e.activation(out=e_[:], in_=lam, func=AF.Exp, scale=-1.0), ["vec"], ["t8e"])
    V(lambda e: e.tensor_scalar(out=w_[:], in0=e_[:], scalar1=1.0, scalar2=None, op0=ALU.add), ["t8e"], ["t8w"])
    A(lambda e: e.activation(out=l_[:], in_=w_[:], func=AF.Ln), ["t8w"], ["t8l"])
    V(lambda e: e.tensor_scalar(out=d_[:], in0=w_[:], scalar1=-1.0, scalar2=None, op0=ALU.add), ["t8w"], ["t8d"])
    V(lambda e: e.tensor_tensor(out=d_[:], in0=e_[:], in1=d_[:], op=ALU.subtract), ["t8e", "t8d"], ["t8d"])
    V(lambda e: e.reciprocal(out=w_[:], in_=w_[:]), ["t8w"], ["t8w"])
    V(lambda e: e.tensor_tensor(out=d_[:], in0=d_[:], in1=w_[:], op=ALU.mult), ["t8d", "t8w"], ["t8d"])
    V(lambda e: e.tensor_tensor(out=l_[:], in0=l_[:], in1=d_[:], op=ALU.add), ["t8l", "t8d"], ["t8l"])
    V(lambda e: e.tensor_scalar(out=c1q[:], in0=l_[:], scalar1=-2.0, scalar2=None, op0=ALU.mult), ["t8l"], ["c1q"])
    V(lambda e: e.tensor_scalar(out=c1h[:], in0=l_[:], scalar1=-4.0, scalar2=None, op0=ALU.mult), ["t8l"], ["c1h"])
    V(lambda e: e.tensor_scalar(out=bah[:], in0=vec[:, VC["ba"]:VC["ba"] + 8], scalar1=0.5, scalar2=None, op0=ALU.mult), ["vec"], ["bah"])
    V(lambda e: e.tensor_scalar(out=bxh[:], in0=vec[:, VC["bx"]:VC["bx"] + 8], scalar1=0.5, scalar2=None, op0=ALU.mult), ["vec"], ["bxh"])

    S.force_small = False
    winb = sb.alloc("winb", [128, 8, WIN_COLS], BF16)
    wab = sb.alloc("wab", [128, 8, 128], BF16)
    wxb = sb.alloc("wxb", [128, 8, 128], BF16)
    xt = [sb.alloc("xt%d" % i, [128, D], F32) for i in range(2)]
    ss = [sb.alloc("ss%d" % i, [128, 1], F32) for i in range(2)]
    rs = [sb.alloc("rs%d" % i, [128, 1], F32) for i in range(2)]
    xn = [sb.alloc("xn%d" % i, [128, D], BF16) for i in range(2)]
    xnT = [sb.alloc("xnT%d" % i, [128, 8, 512], BF16) for i in range(2)]
    xr = [sb.alloc("xr%d" % c, [128, 515], F32) for c in range(8)]
    hbuf = sb.alloc("hbuf", [128, 8, 512], F32)
    NSET = 2
    xa = [sb.alloc("xa%d" % i, [128, 512], F32) for i in range(NSET)]
    xab = [sb.alloc("xab%d" % i, [128, 512], BF16) for i in range(NSET)]
    thr = [sb.alloc("thr%d" % i, [128, 512], F32) for i in range(NSET)]
    thi = [sb.alloc("thi%d" % i, [128, 512], F32) for i in range(NSET)]
    tl = thr
    av = [sb.alloc("av%d" % i, [128, 512], F32) for i in range(NSET)]
    sq = [sb.alloc("sq%d" % i, [128, 512], F32) for i in range(NSET)]
    gt = [sb.alloc("gt%d" % i, [128, 512], F32) for i in range(3)]
    ckvf = sb.alloc("ckvf", [128, 2, 512], F32)
    sqk = sb.alloc("sqk", [128, 2, 512], BF16)
    lat_st = sb.alloc("lat_st", [128, 2, 512], BF16)
    kr_st = sb.alloc("kr_st", [64, 512], BF16)
    cs = sb.alloc("cs", [64, 512], F32)
    sn = sb.alloc("sn", [64, 512], F32)
    ma_st = sb.alloc("ma_st", [128, 8, 512], BF16)
    thb_st = sb.alloc("thb_st", [128, 8, 512], BF16)

    GQ(lambda e: e.dma_start(out=wab[:], in_=lru_wa.rearrange("n c d -> c n d")), [], ["wab"], "wab")
    GQ(lambda e: e.dma_start(out=wxb[:], in_=lru_wx.rearrange("n c d -> c n d")), [], ["wxb"], "wxb")
    win_src = w_in.rearrange("(k p) n -> p k n", p=128)
    WSPL = [(0, 1024), (1024, 1408), (1408, 2432), (2432, 3456), (3456, 4480), (4480, 4736)]
    for (c0, c1) in WSPL:
        for k in range(8):
            GQ(lambda e, k=k, c0=c0, c1=c1: e.dma_start(out=winb[:, k, c0:c1], in_=win_src[:, k, c0:c1]),
               [], ["winb_%d" % c0 if k == 7 else "winb_%d_p%d" % (c0, k)], "winb_%d" % c0)

    def winkeys(col0):
        for (c0, c1) in WSPL:
            if c0 <= col0 < c1:
                return ["winb_%d" % c0] * 8
        raise ValueError

    for c in range(8):
        V(lambda e, c=c: e.memset(xr[c][:, 0:3], 0.0), [], ["xr%d" % c])

    zrot = [0]

    def zbank():
        b = 2 + (zrot[0] % 4)
        zrot[0] += 1
        return b

    def proj(xb, col0, M, bank):
        wk = winkeys(col0)
        for k in range(8):
            T(lambda e, k=k: e.matmul(ps[bank][0:M, :], lhsT=winb[:, k, col0:col0 + M], rhs=xnT[xb][:, k, :],
                                      start=(k == 0), stop=(k == 7)),
              ["xnT%d" % xb, wk[k]], [PK[bank]])

    def rms_latent(xb, col0, norm_col, dst_d, tcol, tag):
        for cc in range(2):
            b = zbank()
            proj(xb, col0 + cc * 128, 128, b)
            A(lambda e, cc=cc, b=b: e.activation(out=ckvf[:, cc, :], in_=ps[b][:], func=AF.Copy), [PK[b]], ["ckvf%d" % cc])
            A(lambda e, cc=cc, b=b: e.activation(out=sqk[:, cc, :], in_=ps[b][:], func=AF.Square), [PK[b]], ["sqk%d" % cc])
        b = zbank()
        for cc in range(2):
            T(lambda e, cc=cc, b=b: e.matmul(ps[b][:], lhsT=ones[:], rhs=sqk[:, cc, :], start=(cc == 0), stop=(cc == 1)),
              ["ones", "sqk%d" % cc], [PK[b]])
        A(lambda e, b=b: e.activation(out=gt[0][:], in_=ps[b][:], func=AF.Sqrt, scale=1.0 / 256, bias=EPS), [PK[b]], ["gt0"])
        V(lambda e: e.reciprocal(out=gt[0][:], in_=gt[0][:]), ["gt0"], ["gt0"])
        for cc in range(2):
            V(lambda e, cc=cc: e.scalar_tensor_tensor(out=lat_st[:, cc, :], in0=ckvf[:, cc, :], scalar=vec[:, norm_col + cc:norm_col + cc + 1],
                                                      in1=gt[0][:], op0=ALU.mult, op1=ALU.mult),
              ["ckvf%d" % cc, "gt0", "vec"], ["lat_st"])
        GQ(lambda e: e.dma_start(out=dst_d[:, :, tcol:tcol + 512], in_=lat_st[:]), ["lat_st"], [tag], "lat_st")

    def prep(t):
        xb = t % 2
        for s in range(4):
            i = s % 2
            r0 = t * 512 + s * 128
            DQ(lambda e, i=i, r0=r0: e.dma_start(out=xt[i][:], in_=x_all[r0:r0 + 128, :]), [], ["xt%d" % i], "xt%d" % i)
            A(lambda e, i=i: e.activation(out=xn[i][:], in_=xt[i][:], func=AF.Square, accum_out=ss[i][:]), ["xt%d" % i], ["xn%d" % i, "ss%d" % i], small=True)
            A(lambda e, i=i: e.activation(out=rs[i][:], in_=ss[i][:], func=AF.Sqrt, scale=1.0 / D, bias=EPS), ["ss%d" % i], ["rs%d" % i], small=True)
            V(lambda e, i=i: e.reciprocal(out=rs[i][:], in_=rs[i][:]), ["rs%d" % i], ["rs%d" % i], small=True)
            A(lambda e, i=i: e.activation(out=xn[i][:], in_=xt[i][:], func=AF.Copy, scale=rs[i][:, 0:1]), ["xt%d" % i, "rs%d" % i], ["xn%d" % i])
            pT = ps[i][:].bitcast(BF16)
            for c in range(8):
                T(lambda e, i=i, c=c, pT=pT: e.transpose(out=pT[:, c * 128:(c + 1) * 128], in_=xn[i][:, c * 128:(c + 1) * 128], identity=ident[:]),
                  ["xn%d" % i, "ident"], [PK[i]])
            V(lambda e, i=i, s=s, pT=pT, xb=xb: e.tensor_tensor(out=xnT[xb][:, :, s * 128:(s + 1) * 128],
                                                         in0=pT.rearrange("p (c t) -> p c t", c=8), in1=gB1[:], op=ALU.mult),
              [PK[i], "gB1"], ["xnT%d" % xb])

    prep(0)
    for t in range(NT):
        owned = (t % 2 == 1)
        xb = t % 2
        if t == 1:
            V(lambda e: e.tensor_scalar(out=hst[:], in0=hst[:], scalar1=vec[:, VC["flag"]:VC["flag"] + 1], scalar2=None, op0=ALU.mult),
              ["hst", "vec"], ["hst"], small=True)

        def conv_stage(c):
            b = zbank()
            proj(xb, c * 128, 128, b)
            q = c % NSET
            A(lambda e, c=c, b=b: e.activation(out=xr[c][:, 3:515], in_=ps[b][:], func=AF.Copy), [PK[b]], ["xr%d" % c])
            cw = VC["cw"]
            dveA.append(lambda c=c, q=q: V(lambda e, c=c, q=q: e.tensor_scalar(out=xa[q][:], in0=xr[c][:, 0:512], scalar1=vec[:, cw + c:cw + c + 1],
                                                  scalar2=vec[:, VC["cb"] + c:VC["cb"] + c + 1], op0=ALU.mult, op1=ALU.add),
              ["xr%d" % c, "vec"], ["xa%d" % q]))
            for k in range(1, 4):
                dveA.append(lambda c=c, q=q, k=k: V(lambda e, c=c, q=q, k=k: e.scalar_tensor_tensor(out=xa[q][:], in0=xr[c][:, k:k + 512], scalar=vec[:, cw + 8 * k + c:cw + 8 * k + c + 1],
                                                                  in1=xa[q][:], op0=ALU.mult, op1=ALU.add),
                  ["xr%d" % c, "vec", "xa%d" % q], ["xa%d" % q]))
            dveA.append(lambda c=c: V(lambda e, c=c: e.tensor_copy(out=xr[c][:, 0:3], in_=xr[c][:, 512:515]), ["xr%d" % c], ["xr%d" % c]))

        def xab_stage(c):
            q = c % NSET
            A(lambda e, q=q: e.activation(out=xab[q][:], in_=xa[q][:], func=AF.Copy), ["xa%d" % q], ["xab%d" % q])

        def lru_stage(c):
            q = c % NSET
            T(lambda e, c=c, q=q: e.matmul(ps[6][:], lhsT=wab[:, c, :], rhs=xab[q][:], start=True, stop=True), ["wab", "xab%d" % q], [PK[6]])
            T(lambda e, c=c, q=q: e.matmul(ps[7][:], lhsT=wxb[:, c, :], rhs=xab[q][:], start=True, stop=True), ["wxb", "xab%d" % q], [PK[7]])
            A(lambda e, c=c, q=q: e.activation(out=thr[q][:], in_=ps[6][:], func=AF.Tanh, scale=0.5, bias=bah[:, c:c + 1]), [PK[6], "bah"], ["thr%d" % q])
            A(lambda e, c=c, q=q: e.activation(out=thi[q][:], in_=ps[7][:], func=AF.Tanh, scale=0.5, bias=bxh[:, c:c + 1]), [PK[7], "bxh"], ["thi%d" % q])
            A(lambda e, c=c, q=q: e.activation(out=av[q][:], in_=thr[q][:], func=AF.Exp, scale=c1h[:, c:c + 1], bias=c1h[:, c:c + 1]),
              ["thr%d" % q, "c1h"], ["av%d" % q])
            A(lambda e, c=c, q=q: e.activation(out=tl[q][:], in_=thr[q][:], func=AF.Tanh, scale=c1q[:, c:c + 1], bias=c1q[:, c:c + 1]),
              ["thr%d" % q, "c1q"], ["thr%d" % q])
            A(lambda e, q=q: e.activation(out=sq[q][:], in_=tl[q][:], func=AF.Sqrt, scale=-0.25), ["thr%d" % q], ["sq%d" % q])
            dveB.append(lambda q=q: V(lambda e, q=q: e.scalar_tensor_tensor(out=thi[q][:], in0=thi[q][:], scalar=1.0, in1=xa[q][:], op0=ALU.add, op1=ALU.mult),
              ["thi%d" % q, "xa%d" % q], ["thi%d" % q]))
            dveB.append(lambda q=q: V(lambda e, q=q: e.scalar_tensor_tensor(out=sq[q][:], in0=av[q][:], scalar=1.0, in1=sq[q][:], op0=ALU.add, op1=ALU.mult),
              ["av%d" % q, "sq%d" % q], ["sq%d" % q]))
            post.append(lambda q=q: G(lambda e, q=q: e.tensor_tensor(out=sq[q][:], in0=sq[q][:], in1=thi[q][:], op=ALU.mult), ["sq%d" % q, "thi%d" % q], ["sq%d" % q]))

        def scan_stage(c):
            q = c % NSET
            dveC.append(lambda c=c, q=q: V(lambda e, c=c, q=q: e.tensor_tensor_scan(out=hbuf[:, c, :], data0=av[q][:], data1=sq[q][:], initial=hst[:, c:c + 1],
                                                       op0=ALU.mult, op1=ALU.add),
              ["av%d" % q, "sq%d" % q, "hst"], ["hbuf%d" % c]))
            dveC.append(lambda c=c: V(lambda e, c=c: e.tensor_copy(out=hst[:, c:c + 1], in_=hbuf[:, c, 511:512]), ["hbuf%d" % c], ["hst"], small=True))

        for c in range(10):
            dveA, dveB, dveC, post = [], [], [], []
            if c < 8:
                conv_stage(c)
            if 1 <= c <= 8:
                lru_stage(c - 1)
            if c >= 2:
                scan_stage(c - 2)
            for ii in range(max(len(dveA), len(dveC))):
                if ii < len(dveA):
                    dveA[ii]()
                if ii < len(dveC):
                    dveC[ii]()
            for fB in dveB:
                fB()
            for pf in post:
                pf()
            if c < 8:
                xab_stage(c)
            if c == 3 and t + 1 < NT:
                prep(t + 1)

        rms_latent(xb, 1024, VC["kvn"], ckvn_d, t * 512, "ckvn_d")
        bA = zbank()
        proj(xb, 1280, 64, bA)
        bB = zbank()
        proj(xb, 1344, 64, bB)
        DQ(lambda e, t=t: e.dma_start(out=cs[:], in_=cos_d[:, t * 512:(t + 1) * 512]), [], ["cs"], "cs")
        DQ(lambda e, t=t: e.dma_start(out=sn[:], in_=sin_d[:, t * 512:(t + 1) * 512]), [], ["sn"], "sn")
        V(lambda e, bA=bA: e.tensor_tensor(out=gt[1][0:64, :], in0=ps[bA][0:64, :], in1=cs[:], op=ALU.mult), [PK[bA], "cs"], ["gt1"])
        V(lambda e, bB=bB: e.tensor_tensor(out=gt[2][0:64, :], in0=ps[bB][0:64, :], in1=sn[:], op=ALU.mult), [PK[bB], "sn"], ["gt2"])
        V(lambda e: e.tensor_tensor(out=kr_st[:], in0=gt[1][0:64, :], in1=gt[2][0:64, :], op=ALU.add), ["gt1", "gt2"], ["kr_st"])
        GQ(lambda e, t=t: e.dma_start(out=krope_d[:, t * 512:(t + 1) * 512], in_=kr_st[:]), ["kr_st"], ["krope_d"], "kr_st")

        if not owned:
            continue
        to = (t // 2) * 512
        for c in range(8):
            bX = zbank()
            proj(xb, 1408 + c * 128, 128, bX)
            A(lambda e, bX=bX: e.activation(out=gt[0][:], in_=ps[bX][:], func=AF.Square), [PK[bX]], ["gt0"])
            V(lambda e: e.tensor_scalar(out=gt[0][:], in0=gt[0][:], scalar1=0.044715, scalar2=1.0, op0=ALU.mult, op1=ALU.add), ["gt0"], ["gt0"])
            V(lambda e, bX=bX: e.tensor_tensor(out=gt[0][:], in0=gt[0][:], in1=ps[bX][:], op=ALU.mult), ["gt0", PK[bX]], ["gt0"])
            A(lambda e: e.activation(out=gt[1][:], in_=gt[0][:], func=AF.Tanh, scale=0.7978845608028654), ["gt0"], ["gt1"])
            V(lambda e, bX=bX: e.scalar_tensor_tensor(out=gt[1][:], in0=gt[1][:], scalar=1.0, in1=ps[bX][:], op0=ALU.add, op1=ALU.mult),
              ["gt1", PK[bX]], ["gt1"])
            bY = zbank()
            proj(xb, 2432 + c * 128, 128, bY)
            A(lambda e, bY=bY: e.activation(out=gt[2][:], in_=ps[bY][:], func=AF.Tanh, scale=0.5), [PK[bY]], ["gt2"])
            V(lambda e: e.scalar_tensor_tensor(out=gt[2][:], in0=gt[2][:], scalar=1.0, in1=gt[1][:], op0=ALU.add, op1=ALU.mult),
              ["gt2", "gt1"], ["gt2"])
            V(lambda e, c=c: e.scalar_tensor_tensor(out=ma_st[:, c, :], in0=gt[2][:], scalar=0.25, in1=hbuf[:, c, :], op0=ALU.mult, op1=ALU.mult),
              ["gt2", "hbuf%d" % c], ["ma_st"])
            bZ = zbank()
            proj(xb, 3456 + c * 128, 128, bZ)
            A(lambda e, c=c, bZ=bZ: e.activation(out=thb_st[:, c, :], in_=ps[bZ][:], func=AF.Tanh, scale=0.5), [PK[bZ]], ["thb_st"])
        GQ(lambda e, to=to: e.dma_start(out=ma_d[:, :, to:to + 512], in_=ma_st[:]), ["ma_st"], ["ma_d"], "ma_st")
        GQ(lambda e, to=to: e.dma_start(out=thb_d[:, :, to:to + 512], in_=thb_st[:]), ["thb_st"], ["thb_d"], "thb_st")
        rms_latent(xb, 4480, VC["qn"], cqn_d, to, "cqn_d")

    phase1_peak = sb.top
    S.barrier()

    sb.top = persist_top
    ckvn = sb.alloc("ckvn", [128, 2, SEQ], BF16)
    krope = sb.alloc("krope", [128, SEQ], BF16)
    cqn = sb.alloc("cqn", [128, 2, OWN], BF16)
    wuqb = sb.alloc("wuqb", [128, 2, 2048], BF16)
    wukvb = sb.alloc("wukvb", [128, 2, 2048], BF16)
    KT = sb.alloc("KT", [128, SEQ], BF16)
    Vt = sb.alloc("Vt", [128, 64, 128], BF16)
    QT = sb.alloc("QT", [128, OWN], BF16)
    QR = sb.alloc("QR", [128, OWN], BF16)
    woutb = sb.alloc("woutb", [128, 8, D], BF16)
    wout_top = sb.top
    NPT = 6
    PT = [sb.alloc("PT%d" % i, [128, 512], BF16) for i in range(NPT)]
    csq = [sb.alloc("csq%d" % i, [64, 512], F32) for i in range(2)]
    snq = [sb.alloc("snq%d" % i, [64, 512], F32) for i in range(2)]
    q1 = [sb.alloc("q1_%d" % i, [64, 512], F32) for i in range(2)]
    q2 = [sb.alloc("q2_%d" % i, [64, 512], F32) for i in range(2)]
    rcp = sb.alloc("rcp", [128, 512], F32)
    accD = sb.alloc("accD", [128, 512], F32)
    accE = sb.alloc("accE", [128, 512], F32)
    accP = sb.alloc("accP", [128, 512], F32)
    ot = sb.alloc("ot", [128, 512], F32)
    thbt = [sb.alloc("thbt%d" % i, [128, 512], BF16) for i in range(2)]
    mat = [sb.alloc("mat%d" % i, [128, 512], BF16) for i in range(2)]
    mgt = [sb.alloc("mgt%d" % i, [128, 512], BF16) for i in range(2)]
    phase2_peak = sb.top

    for cc in range(2):
        DQ(lambda e, cc=cc: e.dma_start(out=ckvn[:, cc, :], in_=ckvn_d[:, cc, :]), ["ckvn_d"], ["ckvn"], "ckvn%d" % cc)
        DQ(lambda e, cc=cc: e.dma_start(out=cqn[:, cc, :], in_=cqn_d[:, cc, :]), ["cqn_d"], ["cqn"], "cqn%d" % cc)
    DQ(lambda e: e.dma_start(out=krope[0:64, :], in_=krope_d), ["krope_d"], ["krope"], "krope")
    V(lambda e: e.memset(krope[64:128, :], 0.0), [], ["kropez"])
    V(lambda e: e.memset(QR[64:128, :], 0.0), [], ["QRz"])
    V(lambda e: e.memset(QR[64:65, :], 1.0), [], ["QRz"])
    GQ(lambda e: e.dma_start(out=krope[64:65, :], in_=kbrow_d), [], ["kropez"], "kbrow")
    for k in range(2):
        GQ(lambda e, k=k: e.dma_start(out=wuqb[:, k, :], in_=w_uq[k * 128:(k + 1) * 128, :]), [], ["wuqb" if k == 1 else "wuqb_p"], "wuqb")
        GQ(lambda e, k=k: e.dma_start(out=wukvb[:, k, :], in_=w_ukv[k * 128:(k + 1) * 128, :]), [], ["wukvb" if k == 1 else "wukvb_p"], "wukvb")
    wout_src = w_out.rearrange("(k p) n -> p k n", p=128)
    for k in range(8):
        GQ(lambda e, k=k: e.dma_start(out=woutb[:, k, :], in_=wout_src[:, k, :]), [], ["woutb" if k == 7 else "woutb_p%d" % k], "woutb")

    SCALE = float(192 ** -0.5)
    prot = [0]

    def pbank():
        b = prot[0] % 4
        prot[0] += 1
        return b

    def qproj(h, j):
        qcol = h * 256
        i2 = j % 2
        DQ(lambda e: e.dma_start(out=csq[i2][:], in_=cos_d[:, (2 * j + 1) * 512:(2 * j + 2) * 512]), [], ["csq%d" % i2], "csq%d" % i2)
        DQ(lambda e: e.dma_start(out=snq[i2][:], in_=sin_d[:, (2 * j + 1) * 512:(2 * j + 2) * 512]), [], ["snq%d" % i2], "snq%d" % i2)
        for k in range(2):
            T(lambda e, k=k: e.matmul(ps[3][:], lhsT=wuqb[:, k, qcol:qcol + 128], rhs=cqn[:, k, j * 512:(j + 1) * 512],
                                      start=(k == 0), stop=(k == 1)), ["wuqb", "cqn"], [PK[3]])
        A(lambda e: e.activation(out=QT[:, j * 512:(j + 1) * 512], in_=ps[3][:], func=AF.Copy), [PK[3]], ["QT%d" % j])
        for k in range(2):
            T(lambda e, k=k: e.matmul(ps[3][0:64, :], lhsT=wuqb[:, k, qcol + 128:qcol + 192], rhs=cqn[:, k, j * 512:(j + 1) * 512],
                                      start=(k == 0), stop=(k == 1)), ["wuqb", "cqn"], [PK[3]])
        V(lambda e: e.tensor_tensor(out=q1[i2][:], in0=ps[3][0:64, :], in1=csq[i2][:], op=ALU.mult), [PK[3], "csq%d" % i2], ["q1_%d" % i2])
        for k in range(2):
            T(lambda e, k=k: e.matmul(ps[3][0:64, :], lhsT=wuqb[:, k, qcol + 192:qcol + 256], rhs=cqn[:, k, j * 512:(j + 1) * 512],
                                      start=(k == 0), stop=(k == 1)), ["wuqb", "cqn"], [PK[3]])
        V(lambda e: e.tensor_tensor(out=q2[i2][:], in0=ps[3][0:64, :], in1=snq[i2][:], op=ALU.mult), [PK[3], "snq%d" % i2], ["q2_%d" % i2])
        V(lambda e: e.tensor_tensor(out=QR[0:64, j * 512:(j + 1) * 512], in0=q1[i2][:], in1=q2[i2][:], op=ALU.add),
          ["q1_%d" % i2, "q2_%d" % i2], ["QR%d" % j])

    fin_i = [0]
    for h in range(8):
        kcol = h * 256
        vcol = h * 256 + 128
        for tt in range(16):
            b = pbank()
            for k in range(2):
                T(lambda e, k=k, tt=tt, b=b, kcol=kcol: e.matmul(ps[b][:], lhsT=wukvb[:, k, kcol:kcol + 128], rhs=ckvn[:, k, tt * 512:(tt + 1) * 512],
                                                      start=(k == 0), stop=(k == 1)), ["wukvb", "ckvn"], [PK[b]])
            if tt % 2 == 0:
                A(lambda e, tt=tt, b=b: e.activation(out=KT[:, tt * 512:(tt + 1) * 512], in_=ps[b][:], func=AF.Copy), [PK[b]], ["KT"])
            else:
                V(lambda e, tt=tt, b=b: e.tensor_copy(out=KT[:, tt * 512:(tt + 1) * 512], in_=ps[b][:]), [PK[b]], ["KT"])
        for g in range(16):
            b = pbank()
            for u in range(4):
                kb = g * 4 + u
                for k in range(2):
                    T(lambda e, k=k, kb=kb, u=u, b=b, vcol=vcol: e.matmul(ps[b][:, u * 128:(u + 1) * 128], lhsT=ckvn[:, k, kb * 128:(kb + 1) * 128],
                                                               rhs=wukvb[:, k, vcol:vcol + 128], start=(k == 0), stop=(k == 1)),
                      ["wukvb", "ckvn"], [PK[b]])
            if g % 2 == 0:
                V(lambda e, g=g, b=b: e.tensor_copy(out=Vt[:, g * 4:(g + 1) * 4, :], in_=ps[b][:].rearrange("p (u d) -> p u d", u=4)), [PK[b]], ["Vt"])
            else:
                A(lambda e, g=g, b=b: e.activation(out=Vt[:, g * 4:(g + 1) * 4, :], in_=ps[b][:].rearrange("p (u d) -> p u d", u=4), func=AF.Copy), [PK[b]], ["Vt"])
        if h == 0:
            qproj(0, 0)

        for j in range(8):
            nkb = 8 * j + 8
            fi = fin_i[0] % 2
            fin_i[0] += 1
            ob = 4 + 2 * fi
            lb = 5 + 2 * fi
            DQ(lambda e, j=j, fi=fi, h=h: e.dma_start(out=thbt[fi][:], in_=thb_d[:, h, j * 512:(j + 1) * 512]), ["thb_d"], ["thbt%d" % fi], "thbt%d" % fi)
            DQ(lambda e, j=j, fi=fi, h=h: e.dma_start(out=mat[fi][:], in_=ma_d[:, h, j * 512:(j + 1) * 512]), ["ma_d"], ["mat%d" % fi], "mat%d" % fi)
            q0 = j * 512

            def qk(kb):
                sbk = kb % 3
                kl = kb - (8 * j + 4)
                c0 = 128 * kl if kl > 0 else 0
                T(lambda e, kb=kb, sbk=sbk, c0=c0, q0=q0: e.matmul(ps[sbk][:, c0:512], lhsT=KT[:, kb * 128:(kb + 1) * 128], rhs=QT[:, q0 + c0:q0 + 512],
                                                            start=True, stop=False), ["KT", "QT%d" % j], [PK[sbk]])
                T(lambda e, kb=kb, sbk=sbk, c0=c0, kl=kl, q0=q0: e.matmul(ps[sbk][:, c0:512], lhsT=krope[:, kb * 128:(kb + 1) * 128], rhs=QR[:, q0 + c0:q0 + 512],
                                                                   start=False, stop=(kl < 0)), ["krope", "QR%d" % j, "kropez", "QRz"], [PK[sbk]])
                if kl >= 0:
                    T(lambda e, sbk=sbk, c0=c0: e.matmul(ps[sbk][:, c0:c0 + 128], lhsT=ident[:], rhs=cmask[:], start=False, stop=True),
                      ["ident", "cmask"], [PK[sbk]])

            def ex(kb):
                sbk = kb % 3
                pi = kb % NPT
                kl = kb - (8 * j + 4)
                c0 = 128 * kl if kl > 0 else 0
                A(lambda e, sbk=sbk, pi=pi, c0=c0: e.activation(out=PT[pi][:, c0:512], in_=ps[sbk][:, c0:512], func=AF.Exp, scale=SCALE),
                  [PK[sbk]], ["PT%d" % pi])

            def pv(kb):
                pi = kb % NPT
                kl = kb - (8 * j + 4)
                c0 = 128 * kl if kl > 0 else 0
                T(lambda e, kb=kb, pi=pi, c0=c0, ob=ob, nkb=nkb: e.matmul(ps[ob][:, c0:512], lhsT=Vt[:, kb, :], rhs=PT[pi][:, c0:512], start=(kb == 0), stop=(kb == nkb - 1)),
                  ["Vt", "PT%d" % pi], [PK[ob]])
                m6 = kb % 12
                if m6 in (0, 4, 8):
                    T(lambda e, kb=kb, pi=pi, c0=c0, lb=lb: e.matmul(ps[lb][:, c0:512], lhsT=ones[:], rhs=PT[pi][:, c0:512], start=(kb == 0), stop=False),
                      ["ones", "PT%d" % pi], [PK[lb]])
                elif m6 in (2, 10):
                    G(lambda e, pi=pi, c0=c0: e.tensor_tensor(out=accP[:, c0:512], in0=PT[pi][:, c0:512], in1=accP[:, c0:512], op=ALU.add),
                      ["PT%d" % pi, "accP"], ["accP"])
                elif m6 in (1, 5, 7, 11):
                    V(lambda e, pi=pi, c0=c0: e.tensor_tensor(out=accD[:, c0:512], in0=PT[pi][:, c0:512], in1=accD[:, c0:512], op=ALU.add),
                      ["PT%d" % pi, "accD"], ["accD"])
                else:
                    V(lambda e, pi=pi, c0=c0: e.tensor_tensor(out=accE[:, c0:512], in0=PT[pi][:, c0:512], in1=accE[:, c0:512], op=ALU.add),
                      ["PT%d" % pi, "accE"], ["accE"])

            V(lambda e: e.memset(accD[:], 0.0), [], ["accD"])
            V(lambda e: e.memset(accE[:], 0.0), [], ["accE"])
            G(lambda e: e.memset(accP[:], 0.0), [], ["accP"])
            LA = 3
            for kk in range(LA):
                qk(kk)
                ex(kk)
            for kb in range(nkb):
                if kb + LA < nkb:
                    qk(kb + LA)
                    ex(kb + LA)
                pv(kb)
                if kb == 4:
                    if j + 1 < 8:
                        qproj(h, j + 1)
                    elif h + 1 < 8:
                        qproj(h + 1, 0)
            V(lambda e: e.tensor_tensor(out=accD[:], in0=accD[:], in1=accP[:], op=ALU.add), ["accD", "accP"], ["accD"])
            V(lambda e: e.tensor_tensor(out=accD[:], in0=accD[:], in1=accE[:], op=ALU.add), ["accD", "accE"], ["accD"])
            T(lambda e, lb=lb: e.matmul(ps[lb][:], lhsT=onesf[:], rhs=accD[:], start=False, stop=True), ["onesf", "accD"], [PK[lb]])
            V(lambda e, lb=lb: e.reciprocal(out=rcp[:], in_=ps[lb][:]), [PK[lb]], ["rcp"])
            V(lambda e, ob=ob: e.tensor_tensor(out=ot[:], in0=ps[ob][:], in1=rcp[:], op=ALU.mult), [PK[ob], "rcp"], ["ot"])
            V(lambda e, fi=fi: e.scalar_tensor_tensor(out=ot[:], in0=thbt[fi][:], scalar=1.0, in1=ot[:], op0=ALU.add, op1=ALU.mult),
              ["thbt%d" % fi, "ot"], ["ot"])
            V(lambda e, fi=fi: e.scalar_tensor_tensor(out=mgt[fi][:], in0=ot[:], scalar=0.5, in1=mat[fi][:], op0=ALU.mult, op1=ALU.add),
              ["ot", "mat%d" % fi], ["mgt%d" % fi])
            GQ(lambda e, j=j, fi=fi, h=h: e.dma_start(out=mg_d[:, h, j * 512:(j + 1) * 512], in_=mgt[fi][:]), ["mgt%d" % fi], ["mg_d"], "mgt%d" % fi)

    S.barrier()

    sb.top = persist_top
    wupb = sb.alloc("wupb", [128, 8, 4096], BF16)
    wdnb = sb.alloc("wdnb", [128, 32, D], BF16)
    wdn_top = sb.top
    assert sb.top <= wout_top - 8 * D * 2, (sb.top, wout_top)
    sb.top = wout_top
    mgin = [sb.alloc("mgin%d" % i, [128, 8, 512], BF16) for i in range(2)]
    xo = [sb.alloc("xo%d" % i, [128, D], F32) for i in range(2)]
    h1t = [sb.alloc("h1t%d" % i, [128, D], F32) for i in range(2)]
    h1n = [sb.alloc("h1n%d" % i, [128, D], BF16) for i in range(2)]
    h1nTs = [sb.alloc("h1nTs%d" % i, [128, 8, 128], BF16) for i in range(2)]
    ss3 = [sb.alloc("ss3_%d" % i, [128, 1], F32) for i in range(2)]
    rs3 = [sb.alloc("rs3_%d" % i, [128, 1], F32) for i in range(2)]
    phase3a_peak = sb.top

    wup_src = w_up.rearrange("(k p) n -> p k n", p=128)
    for k in range(8):
        for c0 in range(0, 4096, 1024):
            GQ(lambda e, k=k, c0=c0: e.dma_start(out=wupb[:, k, c0:c0 + 1024], in_=wup_src[:, k, c0:c0 + 1024]), [],
               ["wupb" if (k == 7 and c0 == 3072) else "wupb_p%d_%d" % (k, c0)], "wupb")
    wdn_src = w_down.rearrange("(k p) n -> p k n", p=128)
    for k in range(32):
        GQ(lambda e, k=k: e.dma_start(out=wdnb[:, k, :], in_=wdn_src[:, k, :]), [], ["wdnb" if k == 31 else "wdnb_p%d" % k], "wdnb")

    for tt in range(8):
        mi = tt % 2
        DQ(lambda e, tt=tt, mi=mi: e.dma_start(out=mgin[mi][:], in_=mg_d[:, :, tt * 512:(tt + 1) * 512]), ["mg_d"], ["mgin%d" % mi], "mgin%d" % mi)
        for s in range(4):
            i = s % 2
            r0 = tt * 512 + s * 128
            DQ(lambda e, i=i, tt=tt, s=s: e.dma_start(out=xo[i][:], in_=x_all[(2 * tt + 1) * 512 + s * 128:(2 * tt + 1) * 512 + s * 128 + 128, :]), [], ["xo%d" % i], "xo%d" % i)
            for hf in range(2):
                b = 2 + hf + 2 * i
                for k in range(8):
                    T(lambda e, k=k, hf=hf, b=b, s=s, mi=mi: e.matmul(ps[b][:], lhsT=mgin[mi][:, k, s * 128:(s + 1) * 128], rhs=woutb[:, k, hf * 512:(hf + 1) * 512],
                                                                      start=(k == 0), stop=(k == 7)), ["mgin%d" % mi, "woutb"], [PK[b]])
                V(lambda e, hf=hf, b=b, i=i: e.tensor_tensor(out=h1t[i][:, hf * 512:(hf + 1) * 512], in0=ps[b][:], in1=xo[i][:, hf * 512:(hf + 1) * 512], op=ALU.add),
                  [PK[b], "xo%d" % i], ["h1t%d" % i])
            GQ(lambda e, i=i, r0=r0: e.dma_start(out=h1_d[r0:r0 + 128, :], in_=h1t[i][:]), ["h1t%d" % i], ["h1_d"], "h1t%d" % i)
            A(lambda e, i=i: e.activation(out=xo[i][:], in_=h1t[i][:], func=AF.Square, accum_out=ss3[i][:]), ["h1t%d" % i], ["xo%d" % i, "ss3_%d" % i], small=True)
            A(lambda e, i=i: e.activation(out=rs3[i][:], in_=ss3[i][:], func=AF.Sqrt, scale=1.0 / D, bias=EPS), ["ss3_%d" % i], ["rs3_%d" % i], small=True)
            V(lambda e, i=i: e.reciprocal(out=rs3[i][:], in_=rs3[i][:]), ["rs3_%d" % i], ["rs3_%d" % i], small=True)
            A(lambda e, i=i: e.activation(out=h1n[i][:], in_=h1t[i][:], func=AF.Copy, scale=rs3[i][:, 0:1]), ["h1t%d" % i, "rs3_%d" % i], ["h1n%d" % i])
            pT = ps[i][:].bitcast(BF16)
            for c in range(8):
                T(lambda e, i=i, c=c, pT=pT: e.transpose(out=pT[:, c * 128:(c + 1) * 128], in_=h1n[i][:, c * 128:(c + 1) * 128], identity=ident[:]),
                  ["h1n%d" % i, "ident"], [PK[i]])
            V(lambda e, i=i, pT=pT: e.tensor_tensor(out=h1nTs[i][:], in0=pT.rearrange("p (c t) -> p c t", c=8), in1=gB2[:], op=ALU.mult),
              [PK[i], "gB2"], ["h1nTs%d" % i])
            GQ(lambda e, i=i, r0=r0: e.dma_start(out=h1nT_d[:, :, r0:r0 + 128], in_=h1nTs[i][:]), ["h1nTs%d" % i], ["h1nT_d"], "h1nTs%d" % i)

    S.barrier()

    sb.top = wdn_top
    hin = [sb.alloc("hin%d" % i, [128, 8, 512], BF16) for i in range(2)]
    actT = sb.alloc("actT", [128, 32, 512], BF16)
    rl = [sb.alloc("rl%d" % i, [128, 512], F32) for i in range(2)]
    h1r = [sb.alloc("h1r%d" % i, [128, D], F32) for i in range(2)]
    op_ = [sb.alloc("op%d" % i, [128, D], F32) for i in range(2)]
    nfb = sb.alloc("nfb", [128, D], F32)
    ss4 = [sb.alloc("ss4_%d" % i, [128, 1], F32) for i in range(2)]
    rs4 = [sb.alloc("rs4_%d" % i, [128, 1], F32) for i in range(2)]
    phase3b_peak = sb.top

    DQ(lambda e: e.dma_start(out=nfb[:], in_=nfb_d), [], ["nfb"], "nfb")
    urot = [0]
    for tt in range(8):
        hi = tt % 2
        DQ(lambda e, tt=tt, hi=hi: e.dma_start(out=hin[hi][:], in_=h1nT_d[:, :, tt * 512:(tt + 1) * 512]), ["h1nT_d"], ["hin%d" % hi], "hin%d" % hi)
        for f in range(32):
            b = urot[0] % 4
            urot[0] += 1
            ri = f % 2
            for k in range(8):
                T(lambda e, k=k, f=f, b=b, hi=hi: e.matmul(ps[b][:], lhsT=wupb[:, k, f * 128:(f + 1) * 128], rhs=hin[hi][:, k, :], start=(k == 0), stop=(k == 7)),
                  ["wupb", "hin%d" % hi], [PK[b]])
            A(lambda e, b=b, ri=ri: e.activation(out=rl[ri][:], in_=ps[b][:], func=AF.Relu), [PK[b]], ["rl%d" % ri])
            V(lambda e, b=b, ri=ri, f=f: e.tensor_tensor(out=actT[:, f, :], in0=ps[b][:], in1=rl[ri][:], op=ALU.mult), [PK[b], "rl%d" % ri], ["actT%d" % f])
        for s in range(4):
            i = s % 2
            r0 = tt * 512 + s * 128
            DQ(lambda e, i=i, r0=r0: e.dma_start(out=h1r[i][:], in_=h1_d[r0:r0 + 128, :]), ["h1_d"], ["h1r%d" % i], "h1r%d" % i)
            for hf in range(2):
                b = 4 + hf + 2 * i
                for f in range(32):
                    T(lambda e, f=f, hf=hf, b=b, s=s: e.matmul(ps[b][:], lhsT=actT[:, f, s * 128:(s + 1) * 128], rhs=wdnb[:, f, hf * 512:(hf + 1) * 512],
                                                               start=(f == 0), stop=(f == 31)), ["actT%d" % f, "wdnb"], [PK[b]])
                V(lambda e, hf=hf, b=b, i=i: e.tensor_tensor(out=op_[i][:, hf * 512:(hf + 1) * 512], in0=ps[b][:], in1=h1r[i][:, hf * 512:(hf + 1) * 512], op=ALU.add),
                  [PK[b], "h1r%d" % i], ["op%d" % i])
            A(lambda e, i=i: e.activation(out=h1r[i][:], in_=op_[i][:], func=AF.Square, accum_out=ss4[i][:]), ["op%d" % i], ["h1r%d" % i, "ss4_%d" % i], small=True)
            A(lambda e, i=i: e.activation(out=rs4[i][:], in_=ss4[i][:], func=AF.Sqrt, scale=1.0 / D, bias=EPS), ["ss4_%d" % i], ["rs4_%d" % i], small=True)
            V(lambda e, i=i: e.reciprocal(out=rs4[i][:], in_=rs4[i][:]), ["rs4_%d" % i], ["rs4_%d" % i], small=True)
            V(lambda e, i=i: e.scalar_tensor_tensor(out=op_[i][:], in0=op_[i][:], scalar=rs4[i][:, 0:1], in1=nfb[:], op0=ALU.mult, op1=ALU.mult),
              ["op%d" % i, "rs4_%d" % i, "nfb"], ["op%d" % i])
            GQ(lambda e, i=i, r0=r0: e.dma_start(out=out[r0:r0 + 128, :], in_=op_[i][:]), ["op%d" % i], ["out"], "op%d" % i)

    S.barrier()
    S.op("sync", lambda e: e.nop())

    S.finalize(nc, st)
    with nc.Block() as block:
        S.emit(block)
    st.close()
    nc._peaks = (phase1_peak, phase2_peak, phase3a_peak, phase3b_peak)
    return nc


def _host_prep(inp):
    f32 = np.float32
    x = np.asarray(inp["x"], f32)
    w_in = np.asarray(inp["w_in"], f32)
    perm = (np.arange(64) + 32) % 64
    o_rx, o_rg, o_cq, o_ckv, o_kr, o_ga, o_gb = 0, 1024, 2048, 2304, 2560, 2624, 3648
    kr = w_in[:, o_kr:o_kr + 64]
    w_in_r = np.concatenate([w_in[:, o_rx:o_rx + 1024], w_in[:, o_ckv:o_ckv + 256], kr, kr[:, perm],
                             w_in[:, o_rg:o_rg + 1024], w_in[:, o_ga:o_ga + 1024], w_in[:, o_gb:o_gb + 1024],
                             w_in[:, o_cq:o_cq + 256]], axis=1)
    w_in_r = np.ascontiguousarray(w_in_r)
    assert w_in_r.shape[1] == WIN_COLS
    w_uq = np.asarray(inp["w_uq"], f32).reshape(256, 8, 192)
    w_uq_r = np.concatenate([w_uq[:, :, :128], w_uq[:, :, 128:], w_uq[:, :, 128:][:, :, perm]], axis=2).reshape(256, 2048)
    w_uq_r = np.ascontiguousarray(w_uq_r)

    def colv(v):
        v = np.asarray(v, f32).reshape(-1, 128)
        return v.T

    vecs_common = np.zeros((128, NV), f32)
    vecs_common[:, VC["nmix"]:VC["nmix"] + 8] = colv(inp["norm_mix"])
    cw = np.asarray(inp["conv_w"], f32)
    for k in range(4):
        vecs_common[:, VC["cw"] + 8 * k:VC["cw"] + 8 * k + 8] = colv(cw[k])
    vecs_common[:, VC["cb"]:VC["cb"] + 8] = colv(inp["conv_b"])
    vecs_common[:, VC["ba"]:VC["ba"] + 8] = colv(np.asarray(inp["lru_ba"]).reshape(-1))
    vecs_common[:, VC["bx"]:VC["bx"] + 8] = colv(np.asarray(inp["lru_bx"]).reshape(-1))
    vecs_common[:, VC["lam"]:VC["lam"] + 8] = colv(inp["lru_lambda"])
    vecs_common[:, VC["qn"]:VC["qn"] + 2] = colv(inp["q_norm"])
    vecs_common[:, VC["kvn"]:VC["kvn"] + 2] = colv(inp["kv_norm"])
    vecs_common[:, VC["nmlp"]:VC["nmlp"] + 8] = colv(inp["norm_mlp"])
    nfb = np.ascontiguousarray(np.broadcast_to(np.asarray(inp["norm_final"], f32)[None, :], (128, D)))

    inv_freq = 1.0 / (10000.0 ** (np.arange(0, 64, 2, dtype=np.float64) / 64.0))
    inv_freq = inv_freq.astype(f32).astype(np.float64)

    def tables(pos):
        ang = (pos.astype(f32)[:, None] * inv_freq.astype(f32)[None, :]).astype(np.float64)
        cos = np.cos(ang)
        sin = np.sin(ang)
        cosT = np.concatenate([cos, cos], -1).T
        sinT = np.concatenate([-sin, sin], -1).T
        return np.ascontiguousarray(cosT.astype(f32)), np.ascontiguousarray(sinT.astype(f32))

    ident = np.eye(128, dtype=f32)
    kk = np.arange(128)[:, None]
    qq = np.arange(128)[None, :]
    cmask = np.where(qq >= kk, 0.0, -30000.0).astype(f32)

    shared = {
        "w_in": w_in_r, "w_uq": w_uq_r, "w_ukv": np.ascontiguousarray(np.asarray(inp["w_ukv"], f32)),
        "w_out": np.ascontiguousarray(np.asarray(inp["w_out"], f32)), "w_up": np.ascontiguousarray(np.asarray(inp["w_up"], f32)),
        "w_down": np.ascontiguousarray(np.asarray(inp["w_down"], f32)),
        "lru_wa": np.ascontiguousarray(np.asarray(inp["lru_wa"], f32)), "lru_wx": np.ascontiguousarray(np.asarray(inp["lru_wx"], f32)),
        "nfb": nfb, "ident": ident, "cmask": cmask,
    }
    in_maps = []
    for core in range(8):
        b, half = core // 2, core % 2
        if half == 1:
            x_all = np.ascontiguousarray(x[b])
            pos = np.arange(SEQ)
            flag, kbias = 1.0, 0.0
        else:
            x_all = np.concatenate([np.zeros((512, D), f32), x[b, :SEQ - 512]], axis=0)
            pos = np.concatenate([np.zeros(512), np.arange(SEQ - 512)])
            flag, kbias = 0.0, -150.0
        cosT, sinT = tables(pos)
        vecs = vecs_common.copy()
        vecs[:, VC["flag"]] = flag
        vecs[:, VC["kbias"]] = kbias
        m = dict(shared)
        kbrow = np.zeros((1, SEQ), f32)
        if half == 0:
            kbrow[0, :512] = kbias / (192 ** -0.5)
        m.update({"x_all": x_all, "vecs": vecs, "cosT": cosT, "sinT": sinT, "kbrow": kbrow})
        in_maps.append(m)
    return in_maps


_NC_CACHE = {}


def kernel(**inputs):
    in_maps = _host_prep(inputs)
    if "nc" not in _NC_CACHE:
        _NC_CACHE["nc"] = build_program()
    nc = _NC_CACHE["nc"]
    res = run_bass_kernel_spmd(nc, in_maps, core_ids=list(range(8)))
    out = np.empty((4, SEQ, D), np.float32)
    for core in range(8):
        b, half = core // 2, core % 2
        ro = res.results[core]["out"]
        for i in range(8):
            out[b, (2 * i + half) * 512:(2 * i + half + 1) * 512] = ro[i * 512:(i + 1) * 512]
    return out
```

```python
import numpy as np
from contextlib import ExitStack
import concourse.bass as bass
import concourse.mybir as mybir
from concourse.bass_utils import run_bass_kernel_spmd

F32 = mybir.dt.float32
BF16 = mybir.dt.bfloat16
AF = mybir.ActivationFunctionType
ALU = mybir.AluOpType

ENGS = ["tensor", "vector", "scalar", "gpsimd", "sync"]
SAME_ENGINE_SYNC = True

SEQ = 8192
OWN = 4096
D = 1024
NT = 16
EPS = 1e-6
WIN_COLS = 4736
SBUF_BASE = 16640
SBUF_LIMIT = 229300


class Op:
    __slots__ = ("eng", "fn", "deps", "dma_key", "signal", "sigval", "idx", "waits", "small")


class Sched:
    def __init__(self):
        self.ops = {e: [] for e in ENGS}
        self.res = {}
        self.dma_count = {}
        self.pending_barrier = {}
        self.all_dma_last = {}
        self.force_small = False

    def op(self, eng, fn, reads=(), writes=(), dma_key=None, small=False):
        o = Op()
        o.small = small or self.force_small
        o.eng = eng
        o.fn = fn
        o.dma_key = dma_key
        o.signal = False
        o.sigval = None
        o.waits = []
        deps = set()
        for k in reads:
            st = self.res.get(k)
            if st is not None and st[0] is not None:
                deps.add(st[0])
        for k in writes:
            st = self.res.get(k)
            if st is not None:
                if st[0] is not None:
                    deps.add(st[0])
                last = {}
                for r in st[1]:
                    if r.dma_key is not None:
                        deps.add(r)
                    else:
                        p = last.get(r.eng)
                        if p is None or p.idx < r.idx:
                            last[r.eng] = r
                deps.update(last.values())
        for k in reads:
            st = self.res.get(k)
            if st is None:
                st = [None, []]
                self.res[k] = st
            st[1].append(o)
        for k in writes:
            self.res[k] = [o, []]
        pb = self.pending_barrier.pop(eng, None)
        if pb:
            deps.update(pb)
        deps.discard(o)
        o.deps = deps
        o.idx = len(self.ops[eng])
        self.ops[eng].append(o)
        if dma_key is not None:
            c = self.dma_count.get(dma_key, 0) + 1
            self.dma_count[dma_key] = c
            o.sigval = 16 * c
            o.signal = True
            self.all_dma_last[dma_key] = o
        return o

    def barrier(self):
        deps = set()
        for e in ENGS:
            for o in reversed(self.ops[e]):
                if o.dma_key is None:
                    deps.add(o)
                    break
        deps.update(self.all_dma_last.values())
        for e in ENGS:
            cur = set(self.pending_barrier.get(e, set())) | deps
            self.pending_barrier[e] = cur

    def finalize(self, nc, stack):
        for e in ENGS:
            for o in self.ops[e]:
                for d in o.deps:
                    if d.dma_key is None:
                        if d.eng == e and (e == "tensor" or not (SAME_ENGINE_SYNC or d.small or o.small)):
                            continue
                        d.signal = True
        sems = {}
        for e in ENGS:
            sems[("eng", e)] = stack.enter_context(nc.semaphore("s_" + e))
        for i, k in enumerate(self.dma_count):
            sems[("dma", k)] = stack.enter_context(nc.semaphore("d%d" % i))
        for e in ENGS:
            c = 0
            for o in self.ops[e]:
                if o.dma_key is None and o.signal:
                    c += 1
                    o.sigval = c
        for e in ENGS:
            known = {}
            for o in self.ops[e]:
                waits = {}
                for d in o.deps:
                    if d.dma_key is None:
                        if d.eng == e and (e == "tensor" or not (SAME_ENGINE_SYNC or d.small or o.small)):
                            continue
                        key = ("eng", d.eng)
                    else:
                        key = ("dma", d.dma_key)
                    v = d.sigval
                    if known.get(key, 0) >= v:
                        continue
                    if waits.get(key, 0) < v:
                        waits[key] = v
                for key, v in waits.items():
                    known[key] = v
                o.waits = [(sems[key], v) for key, v in waits.items()]
        self.sems = sems

    def emit(self, block):
        sems = self.sems

        def run(e):
            def body(eng):
                for o in self.ops[e]:
                    for (s, v) in o.waits:
                        eng.wait_ge(s, v)
                    ins = o.fn(eng)
                    if o.signal:
                        if o.dma_key is not None:
                            ins.then_inc(sems[("dma", o.dma_key)], 16)
                        else:
                            ins.then_inc(sems[("eng", e)], 1)
            return body

        for e in ENGS:
            if self.ops[e]:
                getattr(block, e)(run(e))


class SB:
    def __init__(self, nc):
        self.nc = nc
        self.top = SBUF_BASE
        self.n = 0
        self.peak = 0

    def alloc(self, name, shape, dtype):
        es = 2 if dtype == BF16 else 4
        size = es
        for s in shape[1:]:
            size *= s
        size = (size + 63) // 64 * 64
        assert self.top + size <= SBUF_LIMIT, (name, self.top, size)
        self.n += 1
        t = self.nc.alloc_sbuf_tensor_at("%s_%d" % (name, self.n), list(shape), dtype, offset=self.top)
        self.top += size
        self.peak = max(self.peak, self.top)
        return t


VC = {}
_o = 0
for _n, _w in [("nmix", 8), ("cw", 32), ("cb", 8), ("ba", 8), ("bx", 8), ("lam", 8), ("qn", 2), ("kvn", 2),
               ("nmlp", 8), ("flag", 1), ("kbias", 1)]:
    VC[_n] = _o
    _o += _w
NV = _o


def build_program(debug=False):
    nc = bass.Bass("TRN2", target_bir_lowering=False)

    def din(name, shape, dt=F32):
        return nc.dram_tensor(name, list(shape), dt, kind="ExternalInput").ap()

    def dscr(name, shape, dt):
        kind = "ExternalOutput" if debug else "Internal"
        return nc.dram_tensor(name, list(shape), dt, kind=kind).ap()

    x_all = din("x_all", [SEQ, D])
    w_in = din("w_in", [D, WIN_COLS])
    w_uq = din("w_uq", [256, 2048])
    w_ukv = din("w_ukv", [256, 2048])
    w_out = din("w_out", [D, D])
    w_up = din("w_up", [D, 4096])
    w_down = din("w_down", [4096, D])
    lru_wa = din("lru_wa", [8, 128, 128])
    lru_wx = din("lru_wx", [8, 128, 128])
    vecs_d = din("vecs", [128, NV])
    nfb_d = din("nfb", [128, D])
    cos_d = din("cosT", [64, SEQ])
    sin_d = din("sinT", [64, SEQ])
    ident_d = din("ident", [128, 128])
    cmask_d = din("cmask", [128, 128])
    kbrow_d = din("kbrow", [1, SEQ])
    out = nc.dram_tensor("out", [OWN, D], F32, kind="ExternalOutput").ap()

    ckvn_d = dscr("ckvn_d", [128, 2, SEQ], BF16)
    krope_d = dscr("krope_d", [64, SEQ], BF16)
    cqn_d = dscr("cqn_d", [128, 2, OWN], BF16)
    ma_d = dscr("ma_d", [128, 8, OWN], BF16)
    thb_d = dscr("thb_d", [128, 8, OWN], BF16)
    mg_d = dscr("mg_d", [128, 8, OWN], BF16)
    h1_d = dscr("h1_d", [OWN, D], F32)
    h1nT_d = dscr("h1nT_d", [128, 8, OWN], BF16)

    S = Sched()
    sb = SB(nc)
    st = ExitStack()
    ps = [st.enter_context(nc.psum_tensor("ps%d" % i, [128, 512], F32)) for i in range(8)]
    PK = ["ps%d" % i for i in range(8)]

    def V(fn, r=(), w=(), small=False):
        return S.op("vector", fn, r, w, small=small)

    def A(fn, r=(), w=(), small=False):
        return S.op("scalar", fn, r, w, small=small)

    def T(fn, r=(), w=()):
        return S.op("tensor", fn, r, w)

    def G(fn, r=(), w=()):
        return S.op("gpsimd", fn, r, w)

    def DQ(fn, r, w, key):
        return S.op("sync", fn, r, w, dma_key=key)

    def GQ(fn, r, w, key):
        return S.op("gpsimd", fn, r, w, dma_key=key)

    vec = sb.alloc("vec", [128, NV], F32)
    identf = sb.alloc("identf", [128, 128], F32)
    ident = sb.alloc("ident", [128, 128], BF16)
    cmaskf = sb.alloc("cmaskf", [128, 128], F32)
    cmask = sb.alloc("cmask", [128, 128], BF16)
    ones = sb.alloc("ones", [128, 128], BF16)
    onesf = sb.alloc("onesf", [128, 128], F32)
    gB1 = sb.alloc("gB1", [128, 8, 128], BF16)
    gB2 = sb.alloc("gB2", [128, 8, 128], BF16)
    c1q = sb.alloc("c1q", [128, 8], F32)
    c1h = sb.alloc("c1h", [128, 8], F32)
    bah = sb.alloc("bah", [128, 8], F32)
    bxh = sb.alloc("bxh", [128, 8], F32)
    tmp8 = [sb.alloc("tmp8_%d" % i, [128, 8], F32) for i in range(4)]
    hst = sb.alloc("hst", [128, 8], F32)
    persist_top = sb.top

    S.force_small = True
    DQ(lambda e: e.dma_start(out=vec[:], in_=vecs_d), [], ["vec"], "vec")
    DQ(lambda e: e.dma_start(out=identf[:], in_=ident_d), [], ["identf"], "identf")
    DQ(lambda e: e.dma_start(out=cmaskf[:], in_=cmask_d), [], ["cmaskf"], "cmaskf")
    V(lambda e: e.tensor_copy(out=ident[:], in_=identf[:]), ["identf"], ["ident"])
    V(lambda e: e.tensor_copy(out=cmask[:], in_=cmaskf[:]), ["cmaskf"], ["cmask"])
    V(lambda e: e.memset(onesf[:], 1.0), [], ["onesf"])
    V(lambda e: e.tensor_copy(out=ones[:], in_=onesf[:]), ["onesf"], ["ones"])
    V(lambda e: e.memset(hst[:], 0.0), [], ["hst"])
    for c in range(8):
        V(lambda e, c=c: e.tensor_scalar(out=gB1[:, c, :], in0=onesf[:], scalar1=vec[:, VC["nmix"] + c:VC["nmix"] + c + 1], scalar2=None, op0=ALU.mult),
          ["onesf", "vec"], ["gB1"])
        V(lambda e, c=c: e.tensor_scalar(out=gB2[:, c, :], in0=onesf[:], scalar1=vec[:, VC["nmlp"] + c:VC["nmlp"] + c + 1], scalar2=None, op0=ALU.mult),
          ["onesf", "vec"], ["gB2"])
    lam = vec[:, VC["lam"]:VC["lam"] + 8]
    e_, w_, l_, d_ = tmp8
    A(lambda e: e.activation(out=e_[:], in_=lam, func=AF.Exp, scale=-1.0), ["vec"], ["t8e"])
    V(lambda e: e.tensor_scalar(out=w_[:], in0=e_[:], scalar1=1.0, scalar2=None, op0=ALU.add), ["t8e"], ["t8w"])
    A(lambda e: e.activation(out=l_[:], in_=w_[:], func=AF.Ln), ["t8w"], ["t8l"])
    V(lambda e: e.tensor_scalar(out=d_[:], in0=w_[:], scalar1=-1.0, scalar2=None, op0=ALU.add), ["t8w"], ["t8d"])
    V(lambda e: e.tensor_tensor(out=d_[:], in0=e_[:], in1=d_[:], op=ALU.subtract), ["t8e", "t8d"], ["t8d"])
    V(lambda e: e.reciprocal(out=w_[:], in_=w_[:]), ["t8w"], ["t8w"])
    V(lambda e: e.tensor_tensor(out=d_[:], in0=d_[:], in1=w_[:], op=ALU.mult), ["t8d", "t8w"], ["t8d"])
    V(lambda e: e.tensor_tensor(out=l_[:], in0=l_[:], in1=d_[:], op=ALU.add), ["t8l", "t8d"], ["t8l"])
    V(lambda e: e.tensor_scalar(out=c1q[:], in0=l_[:], scalar1=-2.0, scalar2=None, op0=ALU.mult), ["t8l"], ["c1q"])
    V(lambda e: e.tensor_scalar(out=c1h[:], in0=l_[:], scalar1=-4.0, scalar2=None, op0=ALU.mult), ["t8l"], ["c1h"])
    V(lambda e: e.tensor_scalar(out=bah[:], in0=vec[:, VC["ba"]:VC["ba"] + 8], scalar1=0.5, scalar2=None, op0=ALU.mult), ["vec"], ["bah"])
    V(lambda e: e.tensor_scalar(out=bxh[:], in0=vec[:, VC["bx"]:VC["bx"] + 8], scalar1=0.5, scalar2=None, op0=ALU.mult), ["vec"], ["bxh"])

    S.force_small = False
    winb = sb.alloc("winb", [128, 8, WIN_COLS], BF16)
    wab = sb.alloc("wab", [128, 8, 128], BF16)
    wxb = sb.alloc("wxb", [128, 8, 128], BF16)
    xt = [sb.alloc("xt%d" % i, [128, D], F32) for i in range(2)]
    ss = [sb.alloc("ss%d" % i, [128, 1], F32) for i in range(2)]
    rs = [sb.alloc("rs%d" % i, [128, 1], F32) for i in range(2)]
    xn = [sb.alloc("xn%d" % i, [128, D], BF16) for i in range(2)]
    xnT = [sb.alloc("xnT%d" % i, [128, 8, 512], BF16) for i in range(2)]
    xr = [sb.alloc("xr%d" % c, [128, 515], F32) for c in range(8)]
    hbuf = sb.alloc("hbuf", [128, 8, 512], F32)
    NSET = 2
    xa = [sb.alloc("xa%d" % i, [128, 512], F32) for i in range(NSET)]
    xab = [sb.alloc("xab%d" % i, [128, 512], BF16) for i in range(NSET)]
    thr = [sb.alloc("thr%d" % i, [128, 512], F32) for i in range(NSET)]
    thi = [sb.alloc("thi%d" % i, [128, 512], F32) for i in range(NSET)]
    tl = thr
    av = [sb.alloc("av%d" % i, [128, 512], F32) for i in range(NSET)]
    sq = [sb.alloc("sq%d" % i, [128, 512], F32) for i in range(NSET)]
    gt = [sb.alloc("gt%d" % i, [128, 512], F32) for i in range(3)]
    ckvf = sb.alloc("ckvf", [128, 2, 512], F32)
    sqk = sb.alloc("sqk", [128, 2, 512], BF16)
    lat_st = sb.alloc("lat_st", [128, 2, 512], BF16)
    kr_st = sb.alloc("kr_st", [64, 512], BF16)
    cs = sb.alloc("cs", [64, 512], F32)
    sn = sb.alloc("sn", [64, 512], F32)
    ma_st = sb.alloc("ma_st", [128, 8, 512], BF16)
    thb_st = sb.alloc("thb_st", [128, 8, 512], BF16)

    GQ(lambda e: e.dma_start(out=wab[:], in_=lru_wa.rearrange("n c d -> c n d")), [], ["wab"], "wab")
    GQ(lambda e: e.dma_start(out=wxb[:], in_=lru_wx.rearrange("n c d -> c n d")), [], ["wxb"], "wxb")
    win_src = w_in.rearrange("(k p) n -> p k n", p=128)
    WSPL = [(0, 1024), (1024, 1408), (1408, 2432), (2432, 3456), (3456, 4480), (4480, 4736)]
    for (c0, c1) in WSPL:
        for k in range(8):
            GQ(lambda e, k=k, c0=c0, c1=c1: e.dma_start(out=winb[:, k, c0:c1], in_=win_src[:, k, c0:c1]),
               [], ["winb_%d" % c0 if k == 7 else "winb_%d_p%d" % (c0, k)], "winb_%d" % c0)

    def winkeys(col0):
        for (c0, c1) in WSPL:
            if c0 <= col0 < c1:
                return ["winb_%d" % c0] * 8
        raise ValueError

    for c in range(8):
        V(lambda e, c=c: e.memset(xr[c][:, 0:3], 0.0), [], ["xr%d" % c])

    zrot = [0]

    def zbank():
        b = 2 + (zrot[0] % 4)
        zrot[0] += 1
        return b

    def proj(xb, col0, M, bank):
        wk = winkeys(col0)
        for k in range(8):
            T(lambda e, k=k: e.matmul(ps[bank][0:M, :], lhsT=winb[:, k, col0:col0 + M], rhs=xnT[xb][:, k, :],
                                      start=(k == 0), stop=(k == 7)),
              ["xnT%d" % xb, wk[k]], [PK[bank]])

    def rms_latent(xb, col0, norm_col, dst_d, tcol, tag):
        for cc in range(2):
            b = zbank()
            proj(xb, col0 + cc * 128, 128, b)
            A(lambda e, cc=cc, b=b: e.activation(out=ckvf[:, cc, :], in_=ps[b][:], func=AF.Copy), [PK[b]], ["ckvf%d" % cc])
            A(lambda e, cc=cc, b=b: e.activation(out=sqk[:, cc, :], in_=ps[b][:], func=AF.Square), [PK[b]], ["sqk%d" % cc])
        b = zbank()
        for cc in range(2):
            T(lambda e, cc=cc, b=b: e.matmul(ps[b][:], lhsT=ones[:], rhs=sqk[:, cc, :], start=(cc == 0), stop=(cc == 1)),
              ["ones", "sqk%d" % cc], [PK[b]])
        A(lambda e, b=b: e.activation(out=gt[0][:], in_=ps[b][:], func=AF.Sqrt, scale=1.0 / 256, bias=EPS), [PK[b]], ["gt0"])
        V(lambda e: e.reciprocal(out=gt[0][:], in_=gt[0][:]), ["gt0"], ["gt0"])
        for cc in range(2):
            V(lambda e, cc=cc: e.scalar_tensor_tensor(out=lat_st[:, cc, :], in0=ckvf[:, cc, :], scalar=vec[:, norm_col + cc:norm_col + cc + 1],
                                                      in1=gt[0][:], op0=ALU.mult, op1=ALU.mult),
              ["ckvf%d" % cc, "gt0", "vec"], ["lat_st"])
        GQ(lambda e: e.dma_start(out=dst_d[:, :, tcol:tcol + 512], in_=lat_st[:]), ["lat_st"], [tag], "lat_st")

    def prep(t):
        xb = t % 2
        for s in range(4):
            i = s % 2
            r0 = t * 512 + s * 128
            DQ(lambda e, i=i, r0=r0: e.dma_start(out=xt[i][:], in_=x_all[r0:r0 + 128, :]), [], ["xt%d" % i], "xt%d" % i)
            A(lambda e, i=i: e.activation(out=xn[i][:], in_=xt[i][:], func=AF.Square, accum_out=ss[i][:]), ["xt%d" % i], ["xn%d" % i, "ss%d" % i], small=True)
            A(lambda e, i=i: e.activation(out=rs[i][:], in_=ss[i][:], func=AF.Sqrt, scale=1.0 / D, bias=EPS), ["ss%d" % i], ["rs%d" % i], small=True)
            V(lambda e, i=i: e.reciprocal(out=rs[i][:], in_=rs[i][:]), ["rs%d" % i], ["rs%d" % i], small=True)
            A(lambda e, i=i: e.activation(out=xn[i][:], in_=xt[i][:], func=AF.Copy, scale=rs[i][:, 0:1]), ["xt%d" % i, "rs%d" % i], ["xn%d" % i])
            pT = ps[i][:].bitcast(BF16)
            for c in range(8):
                T(lambda e, i=i, c=c, pT=pT: e.transpose(out=pT[:, c * 128:(c + 1) * 128], in_=xn[i][:, c * 128:(c + 1) * 128], identity=ident[:]),
                  ["xn%d" % i, "ident"], [PK[i]])
            V(lambda e, i=i, s=s, pT=pT, xb=xb: e.tensor_tensor(out=xnT[xb][:, :, s * 128:(s + 1) * 128],
                                                         in0=pT.rearrange("p (c t) -> p c t", c=8), in1=gB1[:], op=ALU.mult),
              [PK[i], "gB1"], ["xnT%d" % xb])

    prep(0)
    for t in range(NT):
        owned = (t % 2 == 1)
        xb = t % 2
        if t == 1:
            V(lambda e: e.tensor_scalar(out=hst[:], in0=hst[:], scalar1=vec[:, VC["flag"]:VC["flag"] + 1], scalar2=None, op0=ALU.mult),
              ["hst", "vec"], ["hst"], small=True)

        def conv_stage(c):
            b = zbank()
            proj(xb, c * 128, 128, b)
            q = c % NSET
            A(lambda e, c=c, b=b: e.activation(out=xr[c][:, 3:515], in_=ps[b][:], func=AF.Copy), [PK[b]], ["xr%d" % c])
            cw = VC["cw"]
            dveA.append(lambda c=c, q=q: V(lambda e, c=c, q=q: e.tensor_scalar(out=xa[q][:], in0=xr[c][:, 0:512], scalar1=vec[:, cw + c:cw + c + 1],
                                                  scalar2=vec[:, VC["cb"] + c:VC["cb"] + c + 1], op0=ALU.mult, op1=ALU.add),
              ["xr%d" % c, "vec"], ["xa%d" % q]))
            for k in range(1, 4):
                dveA.append(lambda c=c, q=q, k=k: V(lambda e, c=c, q=q, k=k: e.scalar_tensor_tensor(out=xa[q][:], in0=xr[c][:, k:k + 512], scalar=vec[:, cw + 8 * k + c:cw + 8 * k + c + 1],
                                                                  in1=xa[q][:], op0=ALU.mult, op1=ALU.add),
                  ["xr%d" % c, "vec", "xa%d" % q], ["xa%d" % q]))
            dveA.append(lambda c=c: V(lambda e, c=c: e.tensor_copy(out=xr[c][:, 0:3], in_=xr[c][:, 512:515]), ["xr%d" % c], ["xr%d" % c]))

        def xab_stage(c):
            q = c % NSET
            A(lambda e, q=q: e.activation(out=xab[q][:], in_=xa[q][:], func=AF.Copy), ["xa%d" % q], ["xab%d" % q])

        def lru_stage(c):
            q = c % NSET
            T(lambda e, c=c, q=q: e.matmul(ps[6][:], lhsT=wab[:, c, :], rhs=xab[q][:], start=True, stop=True), ["wab", "xab%d" % q], [PK[6]])
            T(lambda e, c=c, q=q: e.matmul(ps[7][:], lhsT=wxb[:, c, :], rhs=xab[q][:], start=True, stop=True), ["wxb", "xab%d" % q], [PK[7]])
            A(lambda e, c=c, q=q: e.activation(out=thr[q][:], in_=ps[6][:], func=AF.Tanh, scale=0.5, bias=bah[:, c:c + 1]), [PK[6], "bah"], ["thr%d" % q])
            A(lambda e, c=c, q=q: e.activation(out=thi[q][:], in_=ps[7][:], func=AF.Tanh, scale=0.5, bias=bxh[:, c:c + 1]), [PK[7], "bxh"], ["thi%d" % q])
            A(lambda e, c=c, q=q: e.activation(out=av[q][:], in_=thr[q][:], func=AF.Exp, scale=c1h[:, c:c + 1], bias=c1h[:, c:c + 1]),
              ["thr%d" % q, "c1h"], ["av%d" % q])
            A(lambda e, c=c, q=q: e.activation(out=tl[q][:], in_=thr[q][:], func=AF.Tanh, scale=c1q[:, c:c + 1], bias=c1q[:, c:c + 1]),
              ["thr%d" % q, "c1q"], ["thr%d" % q])
            A(lambda e, q=q: e.activation(out=sq[q][:], in_=tl[q][:], func=AF.Sqrt, scale=-0.25), ["thr%d" % q], ["sq%d" % q])
            dveB.append(lambda q=q: V(lambda e, q=q: e.scalar_tensor_tensor(out=thi[q][:], in0=thi[q][:], scalar=1.0, in1=xa[q][:], op0=ALU.add, op1=ALU.mult),
              ["thi%d" % q, "xa%d" % q], ["thi%d" % q]))
            dveB.append(lambda q=q: V(lambda e, q=q: e.scalar_tensor_tensor(out=sq[q][:], in0=av[q][:], scalar=1.0, in1=sq[q][:], op0=ALU.add, op1=ALU.mult),
              ["av%d" % q, "sq%d" % q], ["sq%d" % q]))
            post.append(lambda q=q: G(lambda e, q=q: e.tensor_tensor(out=sq[q][:], in0=sq[q][:], in1=thi[q][:], op=ALU.mult), ["sq%d" % q, "thi%d" % q], ["sq%d" % q]))

        def scan_stage(c):
            q = c % NSET
            dveC.append(lambda c=c, q=q: V(lambda e, c=c, q=q: e.tensor_tensor_scan(out=hbuf[:, c, :], data0=av[q][:], data1=sq[q][:], initial=hst[:, c:c + 1],
                                                       op0=ALU.mult, op1=ALU.add),
              ["av%d" % q, "sq%d" % q, "hst"], ["hbuf%d" % c]))
            dveC.append(lambda c=c: V(lambda e, c=c: e.tensor_copy(out=hst[:, c:c + 1], in_=hbuf[:, c, 511:512]), ["hbuf%d" % c], ["hst"], small=True))

        for c in range(10):
            dveA, dveB, dveC, post = [], [], [], []
            if c < 8:
                conv_stage(c)
            if 1 <= c <= 8:
                lru_stage(c - 1)
            if c >= 2:
                scan_stage(c - 2)
            for ii in range(max(len(dveA), len(dveC))):
                if ii < len(dveA):
                    dveA[ii]()
                if ii < len(dveC):
                    dveC[ii]()
            for fB in dveB:
                fB()
            for pf in post:
                pf()
            if c < 8:
                xab_stage(c)
            if c == 3 and t + 1 < NT:
                prep(t + 1)

        rms_latent(xb, 1024, VC["kvn"], ckvn_d, t * 512, "ckvn_d")
        bA = zbank()
        proj(xb, 1280, 64, bA)
        bB = zbank()
        proj(xb, 1344, 64, bB)
        DQ(lambda e, t=t: e.dma_start(out=cs[:], in_=cos_d[:, t * 512:(t + 1) * 512]), [], ["cs"], "cs")
        DQ(lambda e, t=t: e.dma_start(out=sn[:], in_=sin_d[:, t * 512:(t + 1) * 512]), [], ["sn"], "sn")
        V(lambda e, bA=bA: e.tensor_tensor(out=gt[1][0:64, :], in0=ps[bA][0:64, :], in1=cs[:], op=ALU.mult), [PK[bA], "cs"], ["gt1"])
        V(lambda e, bB=bB: e.tensor_tensor(out=gt[2][0:64, :], in0=ps[bB][0:64, :], in1=sn[:], op=ALU.mult), [PK[bB], "sn"], ["gt2"])
        V(lambda e: e.tensor_tensor(out=kr_st[:], in0=gt[1][0:64, :], in1=gt[2][0:64, :], op=ALU.add), ["gt1", "gt2"], ["kr_st"])
        GQ(lambda e, t=t: e.dma_start(out=krope_d[:, t * 512:(t + 1) * 512], in_=kr_st[:]), ["kr_st"], ["krope_d"], "kr_st")

        if not owned:
            continue
        to = (t // 2) * 512
        for c in range(8):
            bX = zbank()
            proj(xb, 1408 + c * 128, 128, bX)
            A(lambda e, bX=bX: e.activation(out=gt[0][:], in_=ps[bX][:], func=AF.Square), [PK[bX]], ["gt0"])
            V(lambda e: e.tensor_scalar(out=gt[0][:], in0=gt[0][:], scalar1=0.044715, scalar2=1.0, op0=ALU.mult, op1=ALU.add), ["gt0"], ["gt0"])
            V(lambda e, bX=bX: e.tensor_tensor(out=gt[0][:], in0=gt[0][:], in1=ps[bX][:], op=ALU.mult), ["gt0", PK[bX]], ["gt0"])
            A(lambda e: e.activation(out=gt[1][:], in_=gt[0][:], func=AF.Tanh, scale=0.7978845608028654), ["gt0"], ["gt1"])
            V(lambda e, bX=bX: e.scalar_tensor_tensor(out=gt[1][:], in0=gt[1][:], scalar=1.0, in1=ps[bX][:], op0=ALU.add, op1=ALU.mult),
              ["gt1", PK[bX]], ["gt1"])
            bY = zbank()
            proj(xb, 2432 + c * 128, 128, bY)
            A(lambda e, bY=bY: e.activation(out=gt[2][:], in_=ps[bY][:], func=AF.Tanh, scale=0.5), [PK[bY]], ["gt2"])
            V(lambda e: e.scalar_tensor_tensor(out=gt[2][:], in0=gt[2][:], scalar=1.0, in1=gt[1][:], op0=ALU.add, op1=ALU.mult),
              ["gt2", "gt1"], ["gt2"])
            V(lambda e, c=c: e.scalar_tensor_tensor(out=ma_st[:, c, :], in0=gt[2][:], scalar=0.25, in1=hbuf[:, c, :], op0=ALU.mult, op1=ALU.mult),
              ["gt2", "hbuf%d" % c], ["ma_st"])
            bZ = zbank()
            proj(xb, 3456 + c * 128, 128, bZ)
            A(lambda e, c=c, bZ=bZ: e.activation(out=thb_st[:, c, :], in_=ps[bZ][:], func=AF.Tanh, scale=0.5), [PK[bZ]], ["thb_st"])
        GQ(lambda e, to=to: e.dma_start(out=ma_d[:, :, to:to + 512], in_=ma_st[:]), ["ma_st"], ["ma_d"], "ma_st")
        GQ(lambda e, to=to: e.dma_start(out=thb_d[:, :, to:to + 512], in_=thb_st[:]), ["thb_st"], ["thb_d"], "thb_st")
        rms_latent(xb, 4480, VC["qn"], cqn_d, to, "cqn_d")

    phase1_peak = sb.top
    S.barrier()

    sb.top = persist_top
    ckvn = sb.alloc("ckvn", [128, 2, SEQ], BF16)
    krope = sb.alloc("krope", [128, SEQ], BF16)
    cqn = sb.alloc("cqn", [128, 2, OWN], BF16)
    wuqb = sb.alloc("wuqb", [128, 2, 2048], BF16)
    wukvb = sb.alloc("wukvb", [128, 2, 2048], BF16)
    KT = sb.alloc("KT", [128, SEQ], BF16)
    Vt = sb.alloc("Vt", [128, 64, 128], BF16)
    QT = sb.alloc("QT", [128, OWN], BF16)
    QR = sb.alloc("QR", [128, OWN], BF16)
    woutb = sb.alloc("woutb", [128, 8, D], BF16)
    wout_top = sb.top
    NPT = 8
    PT = [sb.alloc("PT%d" % i, [128, 512], BF16) for i in range(NPT)]
    csq = [sb.alloc("csq%d" % i, [64, 512], F32) for i in range(2)]
    snq = [sb.alloc("snq%d" % i, [64, 512], F32) for i in range(2)]
    q1 = [sb.alloc("q1_%d" % i, [64, 512], F32) for i in range(2)]
    q2 = [sb.alloc("q2_%d" % i, [64, 512], F32) for i in range(2)]
    rcp = sb.alloc("rcp", [128, 512], F32)
    accD = sb.alloc("accD", [128, 512], F32)
    accE = sb.alloc("accE", [128, 512], F32)
    accP = sb.alloc("accP", [128, 512], F32)
    ot = sb.alloc("ot", [128, 512], F32)
    thbt = [sb.alloc("thbt%d" % i, [128, 512], BF16) for i in range(2)]
    mat = [sb.alloc("mat%d" % i, [128, 512], BF16) for i in range(2)]
    mgt = [sb.alloc("mgt%d" % i, [128, 512], BF16) for i in range(2)]
    phase2_peak = sb.top

    for cc in range(2):
        DQ(lambda e, cc=cc: e.dma_start(out=ckvn[:, cc, :], in_=ckvn_d[:, cc, :]), ["ckvn_d"], ["ckvn"], "ckvn%d" % cc)
        DQ(lambda e, cc=cc: e.dma_start(out=cqn[:, cc, :], in_=cqn_d[:, cc, :]), ["cqn_d"], ["cqn"], "cqn%d" % cc)
    DQ(lambda e: e.dma_start(out=krope[0:64, :], in_=krope_d), ["krope_d"], ["krope"], "krope")
    V(lambda e: e.memset(krope[64:128, :], 0.0), [], ["kropez"])
    V(lambda e: e.memset(QR[64:128, :], 0.0), [], ["QRz"])
    V(lambda e: e.memset(QR[64:65, :], 1.0), [], ["QRz"])
    GQ(lambda e: e.dma_start(out=krope[64:65, :], in_=kbrow_d), [], ["kropez"], "kbrow")
    for k in range(2):
        GQ(lambda e, k=k: e.dma_start(out=wuqb[:, k, :], in_=w_uq[k * 128:(k + 1) * 128, :]), [], ["wuqb" if k == 1 else "wuqb_p"], "wuqb")
        GQ(lambda e, k=k: e.dma_start(out=wukvb[:, k, :], in_=w_ukv[k * 128:(k + 1) * 128, :]), [], ["wukvb" if k == 1 else "wukvb_p"], "wukvb")
    wout_src = w_out.rearrange("(k p) n -> p k n", p=128)
    for k in range(8):
        GQ(lambda e, k=k: e.dma_start(out=woutb[:, k, :], in_=wout_src[:, k, :]), [], ["woutb" if k == 7 else "woutb_p%d" % k], "woutb")

    SCALE = float(192 ** -0.5)
    prot = [0]

    def pbank():
        b = prot[0] % 4
        prot[0] += 1
        return b

    def qproj(h, j):
        qcol = h * 256
        i2 = j % 2
        DQ(lambda e: e.dma_start(out=csq[i2][:], in_=cos_d[:, (2 * j + 1) * 512:(2 * j + 2) * 512]), [], ["csq%d" % i2], "csq%d" % i2)
        DQ(lambda e: e.dma_start(out=snq[i2][:], in_=sin_d[:, (2 * j + 1) * 512:(2 * j + 2) * 512]), [], ["snq%d" % i2], "snq%d" % i2)
        for k in range(2):
            T(lambda e, k=k: e.matmul(ps[3][:], lhsT=wuqb[:, k, qcol:qcol + 128], rhs=cqn[:, k, j * 512:(j + 1) * 512],
                                      start=(k == 0), stop=(k == 1)), ["wuqb", "cqn"], [PK[3]])
        A(lambda e: e.activation(out=QT[:, j * 512:(j + 1) * 512], in_=ps[3][:], func=AF.Copy), [PK[3]], ["QT%d" % j])
        for k in range(2):
            T(lambda e, k=k: e.matmul(ps[3][0:64, :], lhsT=wuqb[:, k, qcol + 128:qcol + 192], rhs=cqn[:, k, j * 512:(j + 1) * 512],
                                      start=(k == 0), stop=(k == 1)), ["wuqb", "cqn"], [PK[3]])
        V(lambda e: e.tensor_tensor(out=q1[i2][:], in0=ps[3][0:64, :], in1=csq[i2][:], op=ALU.mult), [PK[3], "csq%d" % i2], ["q1_%d" % i2])
        for k in range(2):
            T(lambda e, k=k: e.matmul(ps[3][0:64, :], lhsT=wuqb[:, k, qcol + 192:qcol + 256], rhs=cqn[:, k, j * 512:(j + 1) * 512],
                                      start=(k == 0), stop=(k == 1)), ["wuqb", "cqn"], [PK[3]])
        V(lambda e: e.tensor_tensor(out=q2[i2][:], in0=ps[3][0:64, :], in1=snq[i2][:], op=ALU.mult), [PK[3], "snq%d" % i2], ["q2_%d" % i2])
        V(lambda e: e.tensor_tensor(out=QR[0:64, j * 512:(j + 1) * 512], in0=q1[i2][:], in1=q2[i2][:], op=ALU.add),
          ["q1_%d" % i2, "q2_%d" % i2], ["QR%d" % j])

    fin_i = [0]
    for h in range(8):
        kcol = h * 256
        vcol = h * 256 + 128
        for tt in range(16):
            b = pbank()
            for k in range(2):
                T(lambda e, k=k, tt=tt, b=b, kcol=kcol: e.matmul(ps[b][:], lhsT=wukvb[:, k, kcol:kcol + 128], rhs=ckvn[:, k, tt * 512:(tt + 1) * 512],
                                                      start=(k == 0), stop=(k == 1)), ["wukvb", "ckvn"], [PK[b]])
            if tt % 2 == 0:
                A(lambda e, tt=tt, b=b: e.activation(out=KT[:, tt * 512:(tt + 1) * 512], in_=ps[b][:], func=AF.Copy), [PK[b]], ["KT"])
            else:
                V(lambda e, tt=tt, b=b: e.tensor_copy(out=KT[:, tt * 512:(tt + 1) * 512], in_=ps[b][:]), [PK[b]], ["KT"])
        for g in range(16):
            b = pbank()
            for u in range(4):
                kb = g * 4 + u
                for k in range(2):
                    T(lambda e, k=k, kb=kb, u=u, b=b, vcol=vcol: e.matmul(ps[b][:, u * 128:(u + 1) * 128], lhsT=ckvn[:, k, kb * 128:(kb + 1) * 128],
                                                               rhs=wukvb[:, k, vcol:vcol + 128], start=(k == 0), stop=(k == 1)),
                      ["wukvb", "ckvn"], [PK[b]])
            if g % 2 == 0:
                V(lambda e, g=g, b=b: e.tensor_copy(out=Vt[:, g * 4:(g + 1) * 4, :], in_=ps[b][:].rearrange("p (u d) -> p u d", u=4)), [PK[b]], ["Vt"])
            else:
                A(lambda e, g=g, b=b: e.activation(out=Vt[:, g * 4:(g + 1) * 4, :], in_=ps[b][:].rearrange("p (u d) -> p u d", u=4), func=AF.Copy), [PK[b]], ["Vt"])
        if h == 0:
            qproj(0, 0)

        for j in range(8):
            nkb = 8 * j + 8
            fi = fin_i[0] % 2
            fin_i[0] += 1
            ob = 4 + 2 * fi
            lb = 5 + 2 * fi
            DQ(lambda e, j=j, fi=fi, h=h: e.dma_start(out=thbt[fi][:], in_=thb_d[:, h, j * 512:(j + 1) * 512]), ["thb_d"], ["thbt%d" % fi], "thbt%d" % fi)
            DQ(lambda e, j=j, fi=fi, h=h: e.dma_start(out=mat[fi][:], in_=ma_d[:, h, j * 512:(j + 1) * 512]), ["ma_d"], ["mat%d" % fi], "mat%d" % fi)
            q0 = j * 512

            def qk(kb):
                sbk = kb % 3
                kl = kb - (8 * j + 4)
                c0 = 128 * kl if kl > 0 else 0
                T(lambda e, kb=kb, sbk=sbk, c0=c0, q0=q0: e.matmul(ps[sbk][:, c0:512], lhsT=KT[:, kb * 128:(kb + 1) * 128], rhs=QT[:, q0 + c0:q0 + 512],
                                                            start=True, stop=False), ["KT", "QT%d" % j], [PK[sbk]])
                T(lambda e, kb=kb, sbk=sbk, c0=c0, kl=kl, q0=q0: e.matmul(ps[sbk][:, c0:512], lhsT=krope[:, kb * 128:(kb + 1) * 128], rhs=QR[:, q0 + c0:q0 + 512],
                                                                   start=False, stop=(kl < 0)), ["krope", "QR%d" % j, "kropez", "QRz"], [PK[sbk]])
                if kl >= 0:
                    T(lambda e, sbk=sbk, c0=c0: e.matmul(ps[sbk][:, c0:c0 + 128], lhsT=ident[:], rhs=cmask[:], start=False, stop=True),
                      ["ident", "cmask"], [PK[sbk]])

            def ex(kb):
                sbk = kb % 3
                pi = kb % NPT
                kl = kb - (8 * j + 4)
                c0 = 128 * kl if kl > 0 else 0
                A(lambda e, sbk=sbk, pi=pi, c0=c0: e.activation(out=PT[pi][:, c0:512], in_=ps[sbk][:, c0:512], func=AF.Exp, scale=SCALE),
                  [PK[sbk]], ["PT%d" % pi])

            def pv(kb):
                pi = kb % NPT
                kl = kb - (8 * j + 4)
                c0 = 128 * kl if kl > 0 else 0
                T(lambda e, kb=kb, pi=pi, c0=c0, ob=ob, nkb=nkb: e.matmul(ps[ob][:, c0:512], lhsT=Vt[:, kb, :], rhs=PT[pi][:, c0:512], start=(kb == 0), stop=(kb == nkb - 1)),
                  ["Vt", "PT%d" % pi], [PK[ob]])
                m6 = kb % 12
                if m6 in (0, 4, 8):
                    T(lambda e, kb=kb, pi=pi, c0=c0, lb=lb: e.matmul(ps[lb][:, c0:512], lhsT=ones[:], rhs=PT[pi][:, c0:512], start=(kb == 0), stop=False),
                      ["ones", "PT%d" % pi], [PK[lb]])
                elif m6 in (2, 10):
                    G(lambda e, pi=pi, c0=c0: e.tensor_tensor(out=accP[:, c0:512], in0=PT[pi][:, c0:512], in1=accP[:, c0:512], op=ALU.add),
                      ["PT%d" % pi, "accP"], ["accP"])
                elif m6 in (1, 5, 7, 11):
                    V(lambda e, pi=pi, c0=c0: e.tensor_tensor(out=accD[:, c0:512], in0=PT[pi][:, c0:512], in1=accD[:, c0:512], op=ALU.add),
                      ["PT%d" % pi, "accD"], ["accD"])
                else:
                    V(lambda e, pi=pi, c0=c0: e.tensor_tensor(out=accE[:, c0:512], in0=PT[pi][:, c0:512], in1=accE[:, c0:512], op=ALU.add),
                      ["PT%d" % pi, "accE"], ["accE"])

            V(lambda e: e.memset(accD[:], 0.0), [], ["accD"])
            V(lambda e: e.memset(accE[:], 0.0), [], ["accE"])
            G(lambda e: e.memset(accP[:], 0.0), [], ["accP"])
            LA = 3
            for kk in range(LA):
                qk(kk)
                ex(kk)
            for kb in range(nkb):
                if kb + LA < nkb:
                    qk(kb + LA)
                    ex(kb + LA)
                pv(kb)
                if kb == 4:
                    if j + 1 < 8:
                        qproj(h, j + 1)
                    elif h + 1 < 8:
                        qproj(h + 1, 0)
            V(lambda e: e.tensor_tensor(out=accD[:], in0=accD[:], in1=accP[:], op=ALU.add), ["accD", "accP"], ["accD"])
            V(lambda e: e.tensor_tensor(out=accD[:], in0=accD[:], in1=accE[:], op=ALU.add), ["accD", "accE"], ["accD"])
            T(lambda e, lb=lb: e.matmul(ps[lb][:], lhsT=onesf[:], rhs=accD[:], start=False, stop=True), ["onesf", "accD"], [PK[lb]])
            V(lambda e, lb=lb: e.reciprocal(out=rcp[:], in_=ps[lb][:]), [PK[lb]], ["rcp"])
            V(lambda e, ob=ob: e.tensor_tensor(out=ot[:], in0=ps[ob][:], in1=rcp[:], op=ALU.mult), [PK[ob], "rcp"], ["ot"])
            V(lambda e, fi=fi: e.scalar_tensor_tensor(out=ot[:], in0=thbt[fi][:], scalar=1.0, in1=ot[:], op0=ALU.add, op1=ALU.mult),
              ["thbt%d" % fi, "ot"], ["ot"])
            V(lambda e, fi=fi: e.scalar_tensor_tensor(out=mgt[fi][:], in0=ot[:], scalar=0.5, in1=mat[fi][:], op0=ALU.mult, op1=ALU.add),
              ["ot", "mat%d" % fi], ["mgt%d" % fi])
            GQ(lambda e, j=j, fi=fi, h=h: e.dma_start(out=mg_d[:, h, j * 512:(j + 1) * 512], in_=mgt[fi][:]), ["mgt%d" % fi], ["mg_d"], "mgt%d" % fi)

    S.barrier()

    sb.top = persist_top
    wupb = sb.alloc("wupb", [128, 8, 4096], BF16)
    wdnb = sb.alloc("wdnb", [128, 32, D], BF16)
    wdn_top = sb.top
    assert sb.top <= wout_top - 8 * D * 2, (sb.top, wout_top)
    sb.top = wout_top
    mgin = [sb.alloc("mgin%d" % i, [128, 8, 512], BF16) for i in range(2)]
    xo = [sb.alloc("xo%d" % i, [128, D], F32) for i in range(2)]
    h1t = [sb.alloc("h1t%d" % i, [128, D], F32) for i in range(2)]
    h1n = [sb.alloc("h1n%d" % i, [128, D], BF16) for i in range(2)]
    h1nTs = [sb.alloc("h1nTs%d" % i, [128, 8, 128], BF16) for i in range(2)]
    ss3 = [sb.alloc("ss3_%d" % i, [128, 1], F32) for i in range(2)]
    rs3 = [sb.alloc("rs3_%d" % i, [128, 1], F32) for i in range(2)]
    phase3a_peak = sb.top

    wup_src = w_up.rearrange("(k p) n -> p k n", p=128)
    for k in range(8):
        for c0 in range(0, 4096, 1024):
            GQ(lambda e, k=k, c0=c0: e.dma_start(out=wupb[:, k, c0:c0 + 1024], in_=wup_src[:, k, c0:c0 + 1024]), [],
               ["wupb" if (k == 7 and c0 == 3072) else "wupb_p%d_%d" % (k, c0)], "wupb")
    wdn_src = w_down.rearrange("(k p) n -> p k n", p=128)
    for k in range(32):
        GQ(lambda e, k=k: e.dma_start(out=wdnb[:, k, :], in_=wdn_src[:, k, :]), [], ["wdnb" if k == 31 else "wdnb_p%d" % k], "wdnb")

    for tt in range(8):
        mi = tt % 2
        DQ(lambda e, tt=tt, mi=mi: e.dma_start(out=mgin[mi][:], in_=mg_d[:, :, tt * 512:(tt + 1) * 512]), ["mg_d"], ["mgin%d" % mi], "mgin%d" % mi)
        for s in range(4):
            i = s % 2
            r0 = tt * 512 + s * 128
            DQ(lambda e, i=i, tt=tt, s=s: e.dma_start(out=xo[i][:], in_=x_all[(2 * tt + 1) * 512 + s * 128:(2 * tt + 1) * 512 + s * 128 + 128, :]), [], ["xo%d" % i], "xo%d" % i)
            for hf in range(2):
                b = 2 + hf + 2 * i
                for k in range(8):
                    T(lambda e, k=k, hf=hf, b=b, s=s, mi=mi: e.matmul(ps[b][:], lhsT=mgin[mi][:, k, s * 128:(s + 1) * 128], rhs=woutb[:, k, hf * 512:(hf + 1) * 512],
                                                                      start=(k == 0), stop=(k == 7)), ["mgin%d" % mi, "woutb"], [PK[b]])
                V(lambda e, hf=hf, b=b, i=i: e.tensor_tensor(out=h1t[i][:, hf * 512:(hf + 1) * 512], in0=ps[b][:], in1=xo[i][:, hf * 512:(hf + 1) * 512], op=ALU.add),
                  [PK[b], "xo%d" % i], ["h1t%d" % i])
            GQ(lambda e, i=i, r0=r0: e.dma_start(out=h1_d[r0:r0 + 128, :], in_=h1t[i][:]), ["h1t%d" % i], ["h1_d"], "h1t%d" % i)
            A(lambda e, i=i: e.activation(out=xo[i][:], in_=h1t[i][:], func=AF.Square, accum_out=ss3[i][:]), ["h1t%d" % i], ["xo%d" % i, "ss3_%d" % i], small=True)
            A(lambda e, i=i: e.activation(out=rs3[i][:], in_=ss3[i][:], func=AF.Sqrt, scale=1.0 / D, bias=EPS), ["ss3_%d" % i], ["rs3_%d" % i], small=True)
            V(lambda e, i=i: e.reciprocal(out=rs3[i][:], in_=rs3[i][:]), ["rs3_%d" % i], ["rs3_%d" % i], small=True)
            A(lambda e, i=i: e.activation(out=h1n[i][:], in_=h1t[i][:], func=AF.Copy, scale=rs3[i][:, 0:1]), ["h1t%d" % i, "rs3_%d" % i], ["h1n%d" % i])
            pT = ps[i][:].bitcast(BF16)
            for c in range(8):
                T(lambda e, i=i, c=c, pT=pT: e.transpose(out=pT[:, c * 128:(c + 1) * 128], in_=h1n[i][:, c * 128:(c + 1) * 128], identity=ident[:]),
                  ["h1n%d" % i, "ident"], [PK[i]])
            V(lambda e, i=i, pT=pT: e.tensor_tensor(out=h1nTs[i][:], in0=pT.rearrange("p (c t) -> p c t", c=8), in1=gB2[:], op=ALU.mult),
              [PK[i], "gB2"], ["h1nTs%d" % i])
            GQ(lambda e, i=i, r0=r0: e.dma_start(out=h1nT_d[:, :, r0:r0 + 128], in_=h1nTs[i][:]), ["h1nTs%d" % i], ["h1nT_d"], "h1nTs%d" % i)

    S.barrier()

    sb.top = wdn_top
    hin = [sb.alloc("hin%d" % i, [128, 8, 512], BF16) for i in range(2)]
    actT = sb.alloc("actT", [128, 32, 512], BF16)
    rl = [sb.alloc("rl%d" % i, [128, 512], F32) for i in range(2)]
    h1r = [sb.alloc("h1r%d" % i, [128, D], F32) for i in range(2)]
    op_ = [sb.alloc("op%d" % i, [128, D], F32) for i in range(2)]
    nfb = sb.alloc("nfb", [128, D], F32)
    ss4 = [sb.alloc("ss4_%d" % i, [128, 1], F32) for i in range(2)]
    rs4 = [sb.alloc("rs4_%d" % i, [128, 1], F32) for i in range(2)]
    phase3b_peak = sb.top

    DQ(lambda e: e.dma_start(out=nfb[:], in_=nfb_d), [], ["nfb"], "nfb")
    urot = [0]
    for tt in range(8):
        hi = tt % 2
        DQ(lambda e, tt=tt, hi=hi: e.dma_start(out=hin[hi][:], in_=h1nT_d[:, :, tt * 512:(tt + 1) * 512]), ["h1nT_d"], ["hin%d" % hi], "hin%d" % hi)
        for f in range(32):
            b = urot[0] % 4
            urot[0] += 1
            ri = f % 2
            for k in range(8):
                T(lambda e, k=k, f=f, b=b, hi=hi: e.matmul(ps[b][:], lhsT=wupb[:, k, f * 128:(f + 1) * 128], rhs=hin[hi][:, k, :], start=(k == 0), stop=(k == 7)),
                  ["wupb", "hin%d" % hi], [PK[b]])
            A(lambda e, b=b, ri=ri: e.activation(out=rl[ri][:], in_=ps[b][:], func=AF.Relu), [PK[b]], ["rl%d" % ri])
            V(lambda e, b=b, ri=ri, f=f: e.tensor_tensor(out=actT[:, f, :], in0=ps[b][:], in1=rl[ri][:], op=ALU.mult), [PK[b], "rl%d" % ri], ["actT%d" % f])
        for s in range(4):
            i = s % 2
            r0 = tt * 512 + s * 128
            DQ(lambda e, i=i, r0=r0: e.dma_start(out=h1r[i][:], in_=h1_d[r0:r0 + 128, :]), ["h1_d"], ["h1r%d" % i], "h1r%d" % i)
            for hf in range(2):
                b = 4 + hf + 2 * i
                for f in range(32):
                    T(lambda e, f=f, hf=hf, b=b, s=s: e.matmul(ps[b][:], lhsT=actT[:, f, s * 128:(s + 1) * 128], rhs=wdnb[:, f, hf * 512:(hf + 1) * 512],
                                                               start=(f == 0), stop=(f == 31)), ["actT%d" % f, "wdnb"], [PK[b]])
                V(lambda e, hf=hf, b=b, i=i: e.tensor_tensor(out=op_[i][:, hf * 512:(hf + 1) * 512], in0=ps[b][:], in1=h1r[i][:, hf * 512:(hf + 1) * 512], op=ALU.add),
                  [PK[b], "h1r%d" % i], ["op%d" % i])
            A(lambda e, i=i: e.activation(out=h1r[i][:], in_=op_[i][:], func=AF.Square, accum_out=ss4[i][:]), ["op%d" % i], ["h1r%d" % i, "ss4_%d" % i], small=True)
            A(lambda e, i=i: e.activation(out=rs4[i][:], in_=ss4[i][:], func=AF.Sqrt, scale=1.0 / D, bias=EPS), ["ss4_%d" % i], ["rs4_%d" % i], small=True)
            V(lambda e, i=i: e.reciprocal(out=rs4[i][:], in_=rs4[i][:]), ["rs4_%d" % i], ["rs4_%d" % i], small=True)
            V(lambda e, i=i: e.scalar_tensor_tensor(out=op_[i][:], in0=op_[i][:], scalar=rs4[i][:, 0:1], in1=nfb[:], op0=ALU.mult, op1=ALU.mult),
              ["op%d" % i, "rs4_%d" % i, "nfb"], ["op%d" % i])
            GQ(lambda e, i=i, r0=r0: e.dma_start(out=out[r0:r0 + 128, :], in_=op_[i][:]), ["op%d" % i], ["out"], "op%d" % i)

    S.barrier()
    S.op("sync", lambda e: e.nop())

    S.finalize(nc, st)
    with nc.Block() as block:
        S.emit(block)
    st.close()
    nc._peaks = (phase1_peak, phase2_peak, phase3a_peak, phase3b_peak)
    return nc


def _host_prep(inp):
    f32 = np.float32
    x = np.asarray(inp["x"], f32)
    w_in = np.asarray(inp["w_in"], f32)
    perm = (np.arange(64) + 32) % 64
    o_rx, o_rg, o_cq, o_ckv, o_kr, o_ga, o_gb = 0, 1024, 2048, 2304, 2560, 2624, 3648
    kr = w_in[:, o_kr:o_kr + 64]
    w_in_r = np.concatenate([w_in[:, o_rx:o_rx + 1024], w_in[:, o_ckv:o_ckv + 256], kr, kr[:, perm],
                             w_in[:, o_rg:o_rg + 1024], w_in[:, o_ga:o_ga + 1024], w_in[:, o_gb:o_gb + 1024],
                             w_in[:, o_cq:o_cq + 256]], axis=1)
    w_in_r = np.ascontiguousarray(w_in_r)
    assert w_in_r.shape[1] == WIN_COLS
    w_uq = np.asarray(inp["w_uq"], f32).reshape(256, 8, 192)
    w_uq_r = np.concatenate([w_uq[:, :, :128], w_uq[:, :, 128:], w_uq[:, :, 128:][:, :, perm]], axis=2).reshape(256, 2048)
    w_uq_r = np.ascontiguousarray(w_uq_r)

    def colv(v):
        v = np.asarray(v, f32).reshape(-1, 128)
        return v.T

    vecs_common = np.zeros((128, NV), f32)
    vecs_common[:, VC["nmix"]:VC["nmix"] + 8] = colv(inp["norm_mix"])
    cw = np.asarray(inp["conv_w"], f32)
    for k in range(4):
        vecs_common[:, VC["cw"] + 8 * k:VC["cw"] + 8 * k + 8] = colv(cw[k])
    vecs_common[:, VC["cb"]:VC["cb"] + 8] = colv(inp["conv_b"])
    vecs_common[:, VC["ba"]:VC["ba"] + 8] = colv(np.asarray(inp["lru_ba"]).reshape(-1))
    vecs_common[:, VC["bx"]:VC["bx"] + 8] = colv(np.asarray(inp["lru_bx"]).reshape(-1))
    vecs_common[:, VC["lam"]:VC["lam"] + 8] = colv(inp["lru_lambda"])
    vecs_common[:, VC["qn"]:VC["qn"] + 2] = colv(inp["q_norm"])
    vecs_common[:, VC["kvn"]:VC["kvn"] + 2] = colv(inp["kv_norm"])
    vecs_common[:, VC["nmlp"]:VC["nmlp"] + 8] = colv(inp["norm_mlp"])
    nfb = np.ascontiguousarray(np.broadcast_to(np.asarray(inp["norm_final"], f32)[None, :], (128, D)))

    inv_freq = 1.0 / (10000.0 ** (np.arange(0, 64, 2, dtype=np.float64) / 64.0))
    inv_freq = inv_freq.astype(f32).astype(np.float64)

    def tables(pos):
        ang = (pos.astype(f32)[:, None] * inv_freq.astype(f32)[None, :]).astype(np.float64)
        cos = np.cos(ang)
        sin = np.sin(ang)
        cosT = np.concatenate([cos, cos], -1).T
        sinT = np.concatenate([-sin, sin], -1).T
        return np.ascontiguousarray(cosT.astype(f32)), np.ascontiguousarray(sinT.astype(f32))

    ident = np.eye(128, dtype=f32)
    kk = np.arange(128)[:, None]
    qq = np.arange(128)[None, :]
    cmask = np.where(qq >= kk, 0.0, -30000.0).astype(f32)

    shared = {
        "w_in": w_in_r, "w_uq": w_uq_r, "w_ukv": np.ascontiguousarray(np.asarray(inp["w_ukv"], f32)),
        "w_out": np.ascontiguousarray(np.asarray(inp["w_out"], f32)), "w_up": np.ascontiguousarray(np.asarray(inp["w_up"], f32)),
        "w_down": np.ascontiguousarray(np.asarray(inp["w_down"], f32)),
        "lru_wa": np.ascontiguousarray(np.asarray(inp["lru_wa"], f32)), "lru_wx": np.ascontiguousarray(np.asarray(inp["lru_wx"], f32)),
        "nfb": nfb, "ident": ident, "cmask": cmask,
    }
    in_maps = []
    for core in range(8):
        b, half = core // 2, core % 2
        if half == 1:
            x_all = np.ascontiguousarray(x[b])
            pos = np.arange(SEQ)
            flag, kbias = 1.0, 0.0
        else:
            x_all = np.concatenate([np.zeros((512, D), f32), x[b, :SEQ - 512]], axis=0)
            pos = np.concatenate([np.zeros(512), np.arange(SEQ - 512)])
            flag, kbias = 0.0, -150.0
        cosT, sinT = tables(pos)
        vecs = vecs_common.copy()
        vecs[:, VC["flag"]] = flag
        vecs[:, VC["kbias"]] = kbias
        m = dict(shared)
        kbrow = np.zeros((1, SEQ), f32)
        if half == 0:
            kbrow[0, :512] = kbias / (192 ** -0.5)
        m.update({"x_all": x_all, "vecs": vecs, "cosT": cosT, "sinT": sinT, "kbrow": kbrow})
        in_maps.append(m)
    return in_maps


_NC_CACHE = {}


def kernel(**inputs):
    in_maps = _host_prep(inputs)
    if "nc" not in _NC_CACHE:
        _NC_CACHE["nc"] = build_program()
    nc = _NC_CACHE["nc"]
    res = run_bass_kernel_spmd(nc, in_maps, core_ids=list(range(8)))
    out = np.empty((4, SEQ, D), np.float32)
    for core in range(8):
        b, half = core // 2, core % 2
        ro = res.results[core]["out"]
        for i in range(8):
            out[b, (2 * i + half) * 512:(2 * i + half + 1) * 512] = ro[i * 512:(i + 1) * 512]
    return out
```

```python
import numpy as np
from contextlib import ExitStack
import concourse.bass as bass
import concourse.mybir as mybir
from concourse.bass_utils import run_bass_kernel_spmd

F32 = mybir.dt.float32
BF16 = mybir.dt.bfloat16
AF = mybir.ActivationFunctionType
ALU = mybir.AluOpType

ENGS = ["tensor", "vector", "scalar", "gpsimd", "sync"]
SAME_ENGINE_SYNC = True

SEQ = 8192
OWN = 4096
D = 1024
NT = 16
EPS = 1e-6
WIN_COLS = 4736
SBUF_BASE = 16640
SBUF_LIMIT = 229300


class Op:
    __slots__ = ("eng", "fn", "deps", "dma_key", "signal", "sigval", "idx", "waits", "small")


class Sched:
    def __init__(self):
        self.ops = {e: [] for e in ENGS}
        self.res = {}
        self.dma_count = {}
        self.pending_barrier = {}
        self.all_dma_last = {}
        self.force_small = False

    def op(self, eng, fn, reads=(), writes=(), dma_key=None, small=False):
        o = Op()
        o.small = small or self.force_small
        o.eng = eng
        o.fn = fn
        o.dma_key = dma_key
        o.signal = False
        o.sigval = None
        o.waits = []
        deps = set()
        for k in reads:
            st = self.res.get(k)
            if st is not None and st[0] is not None:
                deps.add(st[0])
        for k in writes:
            st = self.res.get(k)
            if st is not None:
                if st[0] is not None:
                    deps.add(st[0])
                last = {}
                for r in st[1]:
                    if r.dma_key is not None:
                        deps.add(r)
                    else:
                        p = last.get(r.eng)
                        if p is None or p.idx < r.idx:
                            last[r.eng] = r
                deps.update(last.values())
        for k in reads:
            st = self.res.get(k)
            if st is None:
                st = [None, []]
                self.res[k] = st
            st[1].append(o)
        for k in writes:
            self.res[k] = [o, []]
        pb = self.pending_barrier.pop(eng, None)
        if pb:
            deps.update(pb)
        deps.discard(o)
        o.deps = deps
        o.idx = len(self.ops[eng])
        self.ops[eng].append(o)
        if dma_key is not None:
            c = self.dma_count.get(dma_key, 0) + 1
            self.dma_count[dma_key] = c
            o.sigval = 16 * c
            o.signal = True
            self.all_dma_last[dma_key] = o
        return o

    def barrier(self):
        deps = set()
        for e in ENGS:
            for o in reversed(self.ops[e]):
                if o.dma_key is None:
                    deps.add(o)
                    break
        deps.update(self.all_dma_last.values())
        for e in ENGS:
            cur = set(self.pending_barrier.get(e, set())) | deps
            self.pending_barrier[e] = cur

    def finalize(self, nc, stack):
        for e in ENGS:
            for o in self.ops[e]:
                for d in o.deps:
                    if d.dma_key is None:
                        if d.eng == e and (e == "tensor" or not (SAME_ENGINE_SYNC or d.small or o.small)):
                            continue
                        d.signal = True
        sems = {}
        for e in ENGS:
            sems[("eng", e)] = stack.enter_context(nc.semaphore("s_" + e))
        for i, k in enumerate(self.dma_count):
            sems[("dma", k)] = stack.enter_context(nc.semaphore("d%d" % i))
        for e in ENGS:
            c = 0
            for o in self.ops[e]:
                if o.dma_key is None and o.signal:
                    c += 1
                    o.sigval = c
        for e in ENGS:
            known = {}
            for o in self.ops[e]:
                waits = {}
                for d in o.deps:
                    if d.dma_key is None:
                        if d.eng == e and (e == "tensor" or not (SAME_ENGINE_SYNC or d.small or o.small)):
                            continue
                        key = ("eng", d.eng)
                    else:
                        key = ("dma", d.dma_key)
                    v = d.sigval
                    if known.get(key, 0) >= v:
                        continue
                    if waits.get(key, 0) < v:
                        waits[key] = v
                for key, v in waits.items():
                    known[key] = v
                o.waits = [(sems[key], v) for key, v in waits.items()]
        self.sems = sems

    def emit(self, block):
        sems = self.sems

        def run(e):
            def body(eng):
                for o in self.ops[e]:
                    for (s, v) in o.waits:
                        eng.wait_ge(s, v)
                    ins = o.fn(eng)
                    if o.signal:
                        if o.dma_key is not None:
                            ins.then_inc(sems[("dma", o.dma_key)], 16)
                        else:
                            ins.then_inc(sems[("eng", e)], 1)
            return body

        for e in ENGS:
            if self.ops[e]:
                getattr(block, e)(run(e))


class SB:
    def __init__(self, nc):
        self.nc = nc
        self.top = SBUF_BASE
        self.n = 0
        self.peak = 0

    def alloc(self, name, shape, dtype):
        es = 2 if dtype == BF16 else 4
        size = es
        for s in shape[1:]:
            size *= s
        size = (size + 63) // 64 * 64
        assert self.top + size <= SBUF_LIMIT, (name, self.top, size)
        self.n += 1
        t = self.nc.alloc_sbuf_tensor_at("%s_%d" % (name, self.n), list(shape), dtype, offset=self.top)
        self.top += size
        self.peak = max(self.peak, self.top)
        return t


VC = {}
_o = 0
for _n, _w in [("nmix", 8), ("cw", 32), ("cb", 8), ("ba", 8), ("bx", 8), ("lam", 8), ("qn", 2), ("kvn", 2),
               ("nmlp", 8), ("flag", 1), ("kbias", 1)]:
    VC[_n] = _o
    _o += _w
NV = _o


def build_program(debug=False):
    nc = bass.Bass("TRN2", target_bir_lowering=False)

    def din(name, shape, dt=F32):
        return nc.dram_tensor(name, list(shape), dt, kind="ExternalInput").ap()

    def dscr(name, shape, dt):
        kind = "ExternalOutput" if debug else "Internal"
        return nc.dram_tensor(name, list(shape), dt, kind=kind).ap()

    x_all = din("x_all", [SEQ, D])
    w_in = din("w_in", [D, WIN_COLS])
    w_uq = din("w_uq", [256, 2048])
    w_ukv = din("w_ukv", [256, 2048])
    w_out = din("w_out", [D, D])
    w_up = din("w_up", [D, 4096])
    w_down = din("w_down", [4096, D])
    lru_wa = din("lru_wa", [8, 128, 128])
    lru_wx = din("lru_wx", [8, 128, 128])
    vecs_d = din("vecs", [128, NV])
    nfb_d = din("nfb", [128, D])
    cos_d = din("cosT", [64, SEQ])
    sin_d = din("sinT", [64, SEQ])
    ident_d = din("ident", [128, 128])
    cmask_d = din("cmask", [128, 128])
    kbrow_d = din("kbrow", [1, SEQ])
    out = nc.dram_tensor("out", [OWN, D], F32, kind="ExternalOutput").ap()

    ckvn_d = dscr("ckvn_d", [128, 2, SEQ], BF16)
    krope_d = dscr("krope_d", [64, SEQ], BF16)
    cqn_d = dscr("cqn_d", [128, 2, OWN], BF16)
    ma_d = dscr("ma_d", [128, 8, OWN], BF16)
    thb_d = dscr("thb_d", [128, 8, OWN], BF16)
    mg_d = dscr("mg_d", [128, 8, OWN], BF16)
    h1_d = dscr("h1_d", [OWN, D], F32)
    h1nT_d = dscr("h1nT_d", [128, 8, OWN], BF16)

    S = Sched()
    sb = SB(nc)
    st = ExitStack()
    ps = [st.enter_context(nc.psum_tensor("ps%d" % i, [128, 512], F32)) for i in range(8)]
    PK = ["ps%d" % i for i in range(8)]

    def V(fn, r=(), w=(), small=False):
        return S.op("vector", fn, r, w, small=small)

    def A(fn, r=(), w=(), small=False):
        return S.op("scalar", fn, r, w, small=small)

    def T(fn, r=(), w=()):
        return S.op("tensor", fn, r, w)

    def G(fn, r=(), w=()):
        return S.op("gpsimd", fn, r, w)

    def DQ(fn, r, w, key):
        return S.op("sync", fn, r, w, dma_key=key)

    def GQ(fn, r, w, key):
        return S.op("gpsimd", fn, r, w, dma_key=key)

    vec = sb.alloc("vec", [128, NV], F32)
    identf = sb.alloc("identf", [128, 128], F32)
    ident = sb.alloc("ident", [128, 128], BF16)
    cmaskf = sb.alloc("cmaskf", [128, 128], F32)
    cmask = sb.alloc("cmask", [128, 128], BF16)
    ones = sb.alloc("ones", [128, 128], BF16)
    onesf = sb.alloc("onesf", [128, 128], F32)
    gB1 = sb.alloc("gB1", [128, 8, 128], BF16)
    gB2 = sb.alloc("gB2", [128, 8, 128], BF16)
    c1q = sb.alloc("c1q", [128, 8], F32)
    c1h = sb.alloc("c1h", [128, 8], F32)
    bah = sb.alloc("bah", [128, 8], F32)
    bxh = sb.alloc("bxh", [128, 8], F32)
    tmp8 = [sb.alloc("tmp8_%d" % i, [128, 8], F32) for i in range(4)]
    hst = sb.alloc("hst", [128, 8], F32)
    persist_top = sb.top

    S.force_small = True
    DQ(lambda e: e.dma_start(out=vec[:], in_=vecs_d), [], ["vec"], "vec")
    DQ(lambda e: e.dma_start(out=identf[:], in_=ident_d), [], ["identf"], "identf")
    DQ(lambda e: e.dma_start(out=cmaskf[:], in_=cmask_d), [], ["cmaskf"], "cmaskf")
    V(lambda e: e.tensor_copy(out=ident[:], in_=identf[:]), ["identf"], ["ident"])
    V(lambda e: e.tensor_copy(out=cmask[:], in_=cmaskf[:]), ["cmaskf"], ["cmask"])
    V(lambda e: e.memset(onesf[:], 1.0), [], ["onesf"])
    V(lambda e: e.tensor_copy(out=ones[:], in_=onesf[:]), ["onesf"], ["ones"])
    V(lambda e: e.memset(hst[:], 0.0), [], ["hst"])
    for c in range(8):
        V(lambda e, c=c: e.tensor_scalar(out=gB1[:, c, :], in0=onesf[:], scalar1=vec[:, VC["nmix"] + c:VC["nmix"] + c + 1], scalar2=None, op0=ALU.mult),
          ["onesf", "vec"], ["gB1"])
        V(lambda e, c=c: e.tensor_scalar(out=gB2[:, c, :], in0=onesf[:], scalar1=vec[:, VC["nmlp"] + c:VC["nmlp"] + c + 1], scalar2=None, op0=ALU.mult),
          ["onesf", "vec"], ["gB2"])
    lam = vec[:, VC["lam"]:VC["lam"] + 8]
    e_, w_, l_, d_ = tmp8
    A(lambda e: e.activation(out=e_[:], in_=lam, func=AF.Exp, scale=-1.0), ["vec"], ["t8e"])
    V(lambda e: e.tensor_scalar(out=w_[:], in0=e_[:], scalar1=1.0, scalar2=None, op0=ALU.add), ["t8e"], ["t8w"])
    A(lambda e: e.activation(out=l_[:], in_=w_[:], func=AF.Ln), ["t8w"], ["t8l"])
    V(lambda e: e.tensor_scalar(out=d_[:], in0=w_[:], scalar1=-1.0, scalar2=None, op0=ALU.add), ["t8w"], ["t8d"])
    V(lambda e: e.tensor_tensor(out=d_[:], in0=e_[:], in1=d_[:], op=ALU.subtract), ["t8e", "t8d"], ["t8d"])
    V(lambda e: e.reciprocal(out=w_[:], in_=w_[:]), ["t8w"], ["t8w"])
    V(lambda e: e.tensor_tensor(out=d_[:], in0=d_[:], in1=w_[:], op=ALU.mult), ["t8d", "t8w"], ["t8d"])
    V(lambda e: e.tensor_tensor(out=l_[:], in0=l_[:], in1=d_[:], op=ALU.add), ["t8l", "t8d"], ["t8l"])
    V(lambda e: e.tensor_scalar(out=c1q[:], in0=l_[:], scalar1=-2.0, scalar2=None, op0=ALU.mult), ["t8l"], ["c1q"])
    V(lambda e: e.tensor_scalar(out=c1h[:], in0=l_[:], scalar1=-4.0, scalar2=None, op0=ALU.mult), ["t8l"], ["c1h"])
    V(lambda e: e.tensor_scalar(out=bah[:], in0=vec[:, VC["ba"]:VC["ba"] + 8], scalar1=0.5, scalar2=None, op0=ALU.mult), ["vec"], ["bah"])
    V(lambda e: e.tensor_scalar(out=bxh[:], in0=vec[:, VC["bx"]:VC["bx"] + 8], scalar1=0.5, scalar2=None, op0=ALU.mult), ["vec"], ["bxh"])

    S.force_small = False
    winb = sb.alloc("winb", [128, 8, WIN_COLS], BF16)
    wab = sb.alloc("wab", [128, 8, 128], BF16)
    wxb = sb.alloc("wxb", [128, 8, 128], BF16)
    xt = [sb.alloc("xt%d" % i, [128, D], F32) for i in range(2)]
    ss = [sb.alloc("ss%d" % i, [128, 1], F32) for i in range(2)]
    rs = [sb.alloc("rs%d" % i, [128, 1], F32) for i in range(2)]
    xn = [sb.alloc("xn%d" % i, [128, D], BF16) for i in range(2)]
    xnT = [sb.alloc("xnT%d" % i, [128, 8, 512], BF16) for i in range(2)]
    xr = [sb.alloc("xr%d" % c, [128, 515], F32) for c in range(8)]
    hbuf = sb.alloc("hbuf", [128, 8, 512], F32)
    NSET = 2
    xa = [sb.alloc("xa%d" % i, [128, 512], F32) for i in range(NSET)]
    xab = [sb.alloc("xab%d" % i, [128, 512], BF16) for i in range(NSET)]
    thr = [sb.alloc("thr%d" % i, [128, 512], F32) for i in range(NSET)]
    thi = [sb.alloc("thi%d" % i, [128, 512], F32) for i in range(NSET)]
    tl = thr
    av = [sb.alloc("av%d" % i, [128, 512], F32) for i in range(NSET)]
    sq = [sb.alloc("sq%d" % i, [128, 512], F32) for i in range(NSET)]
    gt = [sb.alloc("gt%d" % i, [128, 512], F32) for i in range(3)]
    ckvf = sb.alloc("ckvf", [128, 2, 512], F32)
    sqk = sb.alloc("sqk", [128, 2, 512], BF16)
    lat_st = sb.alloc("lat_st", [128, 2, 512], BF16)
    kr_st = sb.alloc("kr_st", [64, 512], BF16)
    cs = sb.alloc("cs", [64, 512], F32)
    sn = sb.alloc("sn", [64, 512], F32)
    ma_st = sb.alloc("ma_st", [128, 8, 512], BF16)
    thb_st = sb.alloc("thb_st", [128, 8, 512], BF16)

    GQ(lambda e: e.dma_start(out=wab[:], in_=lru_wa.rearrange("n c d -> c n d")), [], ["wab"], "wab")
    GQ(lambda e: e.dma_start(out=wxb[:], in_=lru_wx.rearrange("n c d -> c n d")), [], ["wxb"], "wxb")
    win_src = w_in.rearrange("(k p) n -> p k n", p=128)
    WSPL = [(0, 1024), (1024, 1408), (1408, 2432), (2432, 3456), (3456, 4480), (4480, 4736)]
    for (c0, c1) in WSPL:
        for k in range(8):
            GQ(lambda e, k=k, c0=c0, c1=c1: e.dma_start(out=winb[:, k, c0:c1], in_=win_src[:, k, c0:c1]),
               [], ["winb_%d" % c0 if k == 7 else "winb_%d_p%d" % (c0, k)], "winb_%d" % c0)

    def winkeys(col0):
        for (c0, c1) in WSPL:
            if c0 <= col0 < c1:
                return ["winb_%d" % c0] * 8
        raise ValueError

    for c in range(8):
        V(lambda e, c=c: e.memset(xr[c][:, 0:3], 0.0), [], ["xr%d" % c])

    zrot = [0]

    def zbank():
        b = 2 + (zrot[0] % 4)
        zrot[0] += 1
        return b

    def proj(xb, col0, M, bank):
        wk = winkeys(col0)
        for k in range(8):
            T(lambda e, k=k: e.matmul(ps[bank][0:M, :], lhsT=winb[:, k, col0:col0 + M], rhs=xnT[xb][:, k, :],
                                      start=(k == 0), stop=(k == 7)),
              ["xnT%d" % xb, wk[k]], [PK[bank]])

    def rms_latent(xb, col0, norm_col, dst_d, tcol, tag):
        for cc in range(2):
            b = zbank()
            proj(xb, col0 + cc * 128, 128, b)
            A(lambda e, cc=cc, b=b: e.activation(out=ckvf[:, cc, :], in_=ps[b][:], func=AF.Copy), [PK[b]], ["ckvf%d" % cc])
            A(lambda e, cc=cc, b=b: e.activation(out=sqk[:, cc, :], in_=ps[b][:], func=AF.Square), [PK[b]], ["sqk%d" % cc])
        b = zbank()
        for cc in range(2):
            T(lambda e, cc=cc, b=b: e.matmul(ps[b][:], lhsT=ones[:], rhs=sqk[:, cc, :], start=(cc == 0), stop=(cc == 1)),
              ["ones", "sqk%d" % cc], [PK[b]])
        A(lambda e, b=b: e.activation(out=gt[0][:], in_=ps[b][:], func=AF.Sqrt, scale=1.0 / 256, bias=EPS), [PK[b]], ["gt0"])
        V(lambda e: e.reciprocal(out=gt[0][:], in_=gt[0][:]), ["gt0"], ["gt0"])
        for cc in range(2):
            V(lambda e, cc=cc: e.scalar_tensor_tensor(out=lat_st[:, cc, :], in0=ckvf[:, cc, :], scalar=vec[:, norm_col + cc:norm_col + cc + 1],
                                                      in1=gt[0][:], op0=ALU.mult, op1=ALU.mult),
              ["ckvf%d" % cc, "gt0", "vec"], ["lat_st"])
        GQ(lambda e: e.dma_start(out=dst_d[:, :, tcol:tcol + 512], in_=lat_st[:]), ["lat_st"], [tag], "lat_st")

    def prep(t):
        xb = t % 2
        for s in range(4):
            i = s % 2
            r0 = t * 512 + s * 128
            DQ(lambda e, i=i, r0=r0: e.dma_start(out=xt[i][:], in_=x_all[r0:r0 + 128, :]), [], ["xt%d" % i], "xt%d" % i)
            A(lambda e, i=i: e.activation(out=xn[i][:], in_=xt[i][:], func=AF.Square, accum_out=ss[i][:]), ["xt%d" % i], ["xn%d" % i, "ss%d" % i], small=True)
            A(lambda e, i=i: e.activation(out=rs[i][:], in_=ss[i][:], func=AF.Sqrt, scale=1.0 / D, bias=EPS), ["ss%d" % i], ["rs%d" % i], small=True)
            V(lambda e, i=i: e.reciprocal(out=rs[i][:], in_=rs[i][:]), ["rs%d" % i], ["rs%d" % i], small=True)
            A(lambda e, i=i: e.activation(out=xn[i][:], in_=xt[i][:], func=AF.Copy, scale=rs[i][:, 0:1]), ["xt%d" % i, "rs%d" % i], ["xn%d" % i])
            pT = ps[i][:].bitcast(BF16)
            for c in range(8):
                T(lambda e, i=i, c=c, pT=pT: e.transpose(out=pT[:, c * 128:(c + 1) * 128], in_=xn[i][:, c * 128:(c + 1) * 128], identity=ident[:]),
                  ["xn%d" % i, "ident"], [PK[i]])
            V(lambda e, i=i, s=s, pT=pT, xb=xb: e.tensor_tensor(out=xnT[xb][:, :, s * 128:(s + 1) * 128],
                                                         in0=pT.rearrange("p (c t) -> p c t", c=8), in1=gB1[:], op=ALU.mult),
              [PK[i], "gB1"], ["xnT%d" % xb])

    prep(0)
    for t in range(NT):
        owned = (t % 2 == 1)
        xb = t % 2
        if t == 1:
            V(lambda e: e.tensor_scalar(out=hst[:], in0=hst[:], scalar1=vec[:, VC["flag"]:VC["flag"] + 1], scalar2=None, op0=ALU.mult),
              ["hst", "vec"], ["hst"], small=True)

        def conv_stage(c):
            b = zbank()
            proj(xb, c * 128, 128, b)
            q = c % NSET
            A(lambda e, c=c, b=b: e.activation(out=xr[c][:, 3:515], in_=ps[b][:], func=AF.Copy), [PK[b]], ["xr%d" % c])
            cw = VC["cw"]
            dveA.append(lambda c=c, q=q: V(lambda e, c=c, q=q: e.tensor_scalar(out=xa[q][:], in0=xr[c][:, 0:512], scalar1=vec[:, cw + c:cw + c + 1],
                                                  scalar2=vec[:, VC["cb"] + c:VC["cb"] + c + 1], op0=ALU.mult, op1=ALU.add),
              ["xr%d" % c, "vec"], ["xa%d" % q]))
            for k in range(1, 4):
                dveA.append(lambda c=c, q=q, k=k: V(lambda e, c=c, q=q, k=k: e.scalar_tensor_tensor(out=xa[q][:], in0=xr[c][:, k:k + 512], scalar=vec[:, cw + 8 * k + c:cw + 8 * k + c + 1],
                                                                  in1=xa[q][:], op0=ALU.mult, op1=ALU.add),
                  ["xr%d" % c, "vec", "xa%d" % q], ["xa%d" % q]))
            dveA.append(lambda c=c: V(lambda e, c=c: e.tensor_copy(out=xr[c][:, 0:3], in_=xr[c][:, 512:515]), ["xr%d" % c], ["xr%d" % c]))

        def xab_stage(c):
            q = c % NSET
            A(lambda e, q=q: e.activation(out=xab[q][:], in_=xa[q][:], func=AF.Copy), ["xa%d" % q], ["xab%d" % q])

        def lru_stage(c):
            q = c % NSET
            T(lambda e, c=c, q=q: e.matmul(ps[6][:], lhsT=wab[:, c, :], rhs=xab[q][:], start=True, stop=True), ["wab", "xab%d" % q], [PK[6]])
            T(lambda e, c=c, q=q: e.matmul(ps[7][:], lhsT=wxb[:, c, :], rhs=xab[q][:], start=True, stop=True), ["wxb", "xab%d" % q], [PK[7]])
            A(lambda e, c=c, q=q: e.activation(out=thr[q][:], in_=ps[6][:], func=AF.Tanh, scale=0.5, bias=bah[:, c:c + 1]), [PK[6], "bah"], ["thr%d" % q])
            A(lambda e, c=c, q=q: e.activation(out=thi[q][:], in_=ps[7][:], func=AF.Tanh, scale=0.5, bias=bxh[:, c:c + 1]), [PK[7], "bxh"], ["thi%d" % q])
            A(lambda e, c=c, q=q: e.activation(out=av[q][:], in_=thr[q][:], func=AF.Exp, scale=c1h[:, c:c + 1], bias=c1h[:, c:c + 1]),
              ["thr%d" % q, "c1h"], ["av%d" % q])
            A(lambda e, c=c, q=q: e.activation(out=tl[q][:], in_=thr[q][:], func=AF.Tanh, scale=c1q[:, c:c + 1], bias=c1q[:, c:c + 1]),
              ["thr%d" % q, "c1q"], ["thr%d" % q])
            A(lambda e, q=q: e.activation(out=sq[q][:], in_=tl[q][:], func=AF.Sqrt, scale=-0.25), ["thr%d" % q], ["sq%d" % q])
            dveB.append(lambda q=q: V(lambda e, q=q: e.scalar_tensor_tensor(out=thi[q][:], in0=thi[q][:], scalar=1.0, in1=xa[q][:], op0=ALU.add, op1=ALU.mult),
              ["thi%d" % q, "xa%d" % q], ["thi%d" % q]))
            dveB.append(lambda q=q: V(lambda e, q=q: e.scalar_tensor_tensor(out=sq[q][:], in0=av[q][:], scalar=1.0, in1=sq[q][:], op0=ALU.add, op1=ALU.mult),
              ["av%d" % q, "sq%d" % q], ["sq%d" % q]))
            post.append(lambda q=q: G(lambda e, q=q: e.tensor_tensor(out=sq[q][:], in0=sq[q][:], in1=thi[q][:], op=ALU.mult), ["sq%d" % q, "thi%d" % q], ["sq%d" % q]))

        def scan_stage(c):
            q = c % NSET
            dveC.append(lambda c=c, q=q: V(lambda e, c=c, q=q: e.tensor_tensor_scan(out=hbuf[:, c, :], data0=av[q][:], data1=sq[q][:], initial=hst[:, c:c + 1],
                                                       op0=ALU.mult, op1=ALU.add),
              ["av%d" % q, "sq%d" % q, "hst"], ["hbuf%d" % c]))
            dveC.append(lambda c=c: V(lambda e, c=c: e.tensor_copy(out=hst[:, c:c + 1], in_=hbuf[:, c, 511:512]), ["hbuf%d" % c], ["hst"], small=True))

        for c in range(10):
            dveA, dveB, dveC, post = [], [], [], []
            if c < 8:
                conv_stage(c)
            if 1 <= c <= 8:
                lru_stage(c - 1)
            if c >= 2:
                scan_stage(c - 2)
            for ii in range(max(len(dveA), len(dveC))):
                if ii < len(dveA):
                    dveA[ii]()
                if ii < len(dveC):
                    dveC[ii]()
            for fB in dveB:
                fB()
            for pf in post:
                pf()
            if c < 8:
                xab_stage(c)
            if c == 3 and t + 1 < NT:
                prep(t + 1)

        rms_latent(xb, 1024, VC["kvn"], ckvn_d, t * 512, "ckvn_d")
        bA = zbank()
        proj(xb, 1280, 64, bA)
        bB = zbank()
        proj(xb, 1344, 64, bB)
        DQ(lambda e, t=t: e.dma_start(out=cs[:], in_=cos_d[:, t * 512:(t + 1) * 512]), [], ["cs"], "cs")
        DQ(lambda e, t=t: e.dma_start(out=sn[:], in_=sin_d[:, t * 512:(t + 1) * 512]), [], ["sn"], "sn")
        V(lambda e, bA=bA: e.tensor_tensor(out=gt[1][0:64, :], in0=ps[bA][0:64, :], in1=cs[:], op=ALU.mult), [PK[bA], "cs"], ["gt1"])
        V(lambda e, bB=bB: e.tensor_tensor(out=gt[2][0:64, :], in0=ps[bB][0:64, :], in1=sn[:], op=ALU.mult), [PK[bB], "sn"], ["gt2"])
        V(lambda e: e.tensor_tensor(out=kr_st[:], in0=gt[1][0:64, :], in1=gt[2][0:64, :], op=ALU.add), ["gt1", "gt2"], ["kr_st"])
        GQ(lambda e, t=t: e.dma_start(out=krope_d[:, t * 512:(t + 1) * 512], in_=kr_st[:]), ["kr_st"], ["krope_d"], "kr_st")

        if not owned:
            continue
        to = (t // 2) * 512
        for c in range(8):
            bX = zbank()
            proj(xb, 1408 + c * 128, 128, bX)
            A(lambda e, bX=bX: e.activation(out=gt[0][:], in_=ps[bX][:], func=AF.Square), [PK[bX]], ["gt0"])
            V(lambda e: e.tensor_scalar(out=gt[0][:], in0=gt[0][:], scalar1=0.044715, scalar2=1.0, op0=ALU.mult, op1=ALU.add), ["gt0"], ["gt0"])
            V(lambda e, bX=bX: e.tensor_tensor(out=gt[0][:], in0=gt[0][:], in1=ps[bX][:], op=ALU.mult), ["gt0", PK[bX]], ["gt0"])
            A(lambda e: e.activation(out=gt[1][:], in_=gt[0][:], func=AF.Tanh, scale=0.7978845608028654), ["gt0"], ["gt1"])
            V(lambda e, bX=bX: e.scalar_tensor_tensor(out=gt[1][:], in0=gt[1][:], scalar=1.0, in1=ps[bX][:], op0=ALU.add, op1=ALU.mult),
              ["gt1", PK[bX]], ["gt1"])
            bY = zbank()
            proj(xb, 2432 + c * 128, 128, bY)
            A(lambda e, bY=bY: e.activation(out=gt[2][:], in_=ps[bY][:], func=AF.Tanh, scale=0.5), [PK[bY]], ["gt2"])
            V(lambda e: e.scalar_tensor_tensor(out=gt[2][:], in0=gt[2][:], scalar=1.0, in1=gt[1][:], op0=ALU.add, op1=ALU.mult),
              ["gt2", "gt1"], ["gt2"])
            V(lambda e, c=c: e.scalar_tensor_tensor(out=ma_st[:, c, :], in0=gt[2][:], scalar=0.25, in1=hbuf[:, c, :], op0=ALU.mult, op1=ALU.mult),
              ["gt2", "hbuf%d" % c], ["ma_st"])
            bZ = zbank()
            proj(xb, 3456 + c * 128, 128, bZ)
            A(lambda e, c=c, bZ=bZ: e.activation(out=thb_st[:, c, :], in_=ps[bZ][:], func=AF.Tanh, scale=0.5), [PK[bZ]], ["thb_st"])
        GQ(lambda e, to=to: e.dma_start(out=ma_d[:, :, to:to + 512], in_=ma_st[:]), ["ma_st"], ["ma_d"], "ma_st")
        GQ(lambda e, to=to: e.dma_start(out=thb_d[:, :, to:to + 512], in_=thb_st[:]), ["thb_st"], ["thb_d"], "thb_st")
        rms_latent(xb, 4480, VC["qn"], cqn_d, to, "cqn_d")

    phase1_peak = sb.top
    S.barrier()

    sb.top = persist_top
    ckvn = sb.alloc("ckvn", [128, 2, SEQ], BF16)
    krope = sb.alloc("krope", [128, SEQ], BF16)
    cqn = sb.alloc("cqn", [128, 2, OWN], BF16)
    wuqb = sb.alloc("wuqb", [128, 2, 2048], BF16)
    wukvb = sb.alloc("wukvb", [128, 2, 2048], BF16)
    KT = sb.alloc("KT", [128, SEQ], BF16)
    Vt = sb.alloc("Vt", [128, 64, 128], BF16)
    QT = sb.alloc("QT", [128, OWN], BF16)
    QR = sb.alloc("QR", [128, OWN], BF16)
    woutb = sb.alloc("woutb", [128, 8, D], BF16)
    wout_top = sb.top
    NPT = 10
    PT = [sb.alloc("PT%d" % i, [128, 512], BF16) for i in range(NPT)]
    csq = [sb.alloc("csq%d" % i, [64, 512], F32) for i in range(2)]
    snq = [sb.alloc("snq%d" % i, [64, 512], F32) for i in range(2)]
    q1 = [sb.alloc("q1_%d" % i, [64, 512], F32) for i in range(2)]
    q2 = [sb.alloc("q2_%d" % i, [64, 512], F32) for i in range(2)]
    rcp = sb.alloc("rcp", [128, 512], F32)
    accD = sb.alloc("accD", [128, 512], F32)
    accE = sb.alloc("accE", [128, 512], F32)
    accP = sb.alloc("accP", [128, 512], F32)
    ot = sb.alloc("ot", [128, 512], F32)
    thbt = [sb.alloc("thbt%d" % i, [128, 512], BF16) for i in range(2)]
    mat = [sb.alloc("mat%d" % i, [128, 512], BF16) for i in range(2)]
    mgt = [sb.alloc("mgt%d" % i, [128, 512], BF16) for i in range(2)]
    phase2_peak = sb.top

    for cc in range(2):
        DQ(lambda e, cc=cc: e.dma_start(out=ckvn[:, cc, :], in_=ckvn_d[:, cc, :]), ["ckvn_d"], ["ckvn"], "ckvn%d" % cc)
        DQ(lambda e, cc=cc: e.dma_start(out=cqn[:, cc, :], in_=cqn_d[:, cc, :]), ["cqn_d"], ["cqn"], "cqn%d" % cc)
    DQ(lambda e: e.dma_start(out=krope[0:64, :], in_=krope_d), ["krope_d"], ["krope"], "krope")
    V(lambda e: e.memset(krope[64:128, :], 0.0), [], ["kropez"])
    V(lambda e: e.memset(QR[64:128, :], 0.0), [], ["QRz"])
    V(lambda e: e.memset(QR[64:65, :], 1.0), [], ["QRz"])
    GQ(lambda e: e.dma_start(out=krope[64:65, :], in_=kbrow_d), [], ["kropez"], "kbrow")
    for k in range(2):
        GQ(lambda e, k=k: e.dma_start(out=wuqb[:, k, :], in_=w_uq[k * 128:(k + 1) * 128, :]), [], ["wuqb" if k == 1 else "wuqb_p"], "wuqb")
        GQ(lambda e, k=k: e.dma_start(out=wukvb[:, k, :], in_=w_ukv[k * 128:(k + 1) * 128, :]), [], ["wukvb" if k == 1 else "wukvb_p"], "wukvb")
    wout_src = w_out.rearrange("(k p) n -> p k n", p=128)
    for k in range(8):
        GQ(lambda e, k=k: e.dma_start(out=woutb[:, k, :], in_=wout_src[:, k, :]), [], ["woutb" if k == 7 else "woutb_p%d" % k], "woutb")

    SCALE = float(192 ** -0.5)
    prot = [0]

    def pbank():
        b = prot[0] % 4
        prot[0] += 1
        return b

    def qproj(h, j):
        qcol = h * 256
        i2 = j % 2
        DQ(lambda e: e.dma_start(out=csq[i2][:], in_=cos_d[:, (2 * j + 1) * 512:(2 * j + 2) * 512]), [], ["csq%d" % i2], "csq%d" % i2)
        DQ(lambda e: e.dma_start(out=snq[i2][:], in_=sin_d[:, (2 * j + 1) * 512:(2 * j + 2) * 512]), [], ["snq%d" % i2], "snq%d" % i2)
        for k in range(2):
            T(lambda e, k=k: e.matmul(ps[3][:], lhsT=wuqb[:, k, qcol:qcol + 128], rhs=cqn[:, k, j * 512:(j + 1) * 512],
                                      start=(k == 0), stop=(k == 1)), ["wuqb", "cqn"], [PK[3]])
        A(lambda e: e.activation(out=QT[:, j * 512:(j + 1) * 512], in_=ps[3][:], func=AF.Copy), [PK[3]], ["QT%d" % j])
        for k in range(2):
            T(lambda e, k=k: e.matmul(ps[3][0:64, :], lhsT=wuqb[:, k, qcol + 128:qcol + 192], rhs=cqn[:, k, j * 512:(j + 1) * 512],
                                      start=(k == 0), stop=(k == 1)), ["wuqb", "cqn"], [PK[3]])
        V(lambda e: e.tensor_tensor(out=q1[i2][:], in0=ps[3][0:64, :], in1=csq[i2][:], op=ALU.mult), [PK[3], "csq%d" % i2], ["q1_%d" % i2])
        for k in range(2):
            T(lambda e, k=k: e.matmul(ps[3][0:64, :], lhsT=wuqb[:, k, qcol + 192:qcol + 256], rhs=cqn[:, k, j * 512:(j + 1) * 512],
                                      start=(k == 0), stop=(k == 1)), ["wuqb", "cqn"], [PK[3]])
        V(lambda e: e.tensor_tensor(out=q2[i2][:], in0=ps[3][0:64, :], in1=snq[i2][:], op=ALU.mult), [PK[3], "snq%d" % i2], ["q2_%d" % i2])
        V(lambda e: e.tensor_tensor(out=QR[0:64, j * 512:(j + 1) * 512], in0=q1[i2][:], in1=q2[i2][:], op=ALU.add),
          ["q1_%d" % i2, "q2_%d" % i2], ["QR%d" % j])

    fin_i = [0]
    for h in range(8):
        kcol = h * 256
        vcol = h * 256 + 128
        for tt in range(16):
            b = pbank()
            for k in range(2):
                T(lambda e, k=k, tt=tt, b=b, kcol=kcol: e.matmul(ps[b][:], lhsT=wukvb[:, k, kcol:kcol + 128], rhs=ckvn[:, k, tt * 512:(tt + 1) * 512],
                                                      start=(k == 0), stop=(k == 1)), ["wukvb", "ckvn"], [PK[b]])
            if tt % 2 == 0:
                A(lambda e, tt=tt, b=b: e.activation(out=KT[:, tt * 512:(tt + 1) * 512], in_=ps[b][:], func=AF.Copy), [PK[b]], ["KT"])
            else:
                V(lambda e, tt=tt, b=b: e.tensor_copy(out=KT[:, tt * 512:(tt + 1) * 512], in_=ps[b][:]), [PK[b]], ["KT"])
        for g in range(16):
            b = pbank()
            for u in range(4):
                kb = g * 4 + u
                for k in range(2):
                    T(lambda e, k=k, kb=kb, u=u, b=b, vcol=vcol: e.matmul(ps[b][:, u * 128:(u + 1) * 128], lhsT=ckvn[:, k, kb * 128:(kb + 1) * 128],
                                                               rhs=wukvb[:, k, vcol:vcol + 128], start=(k == 0), stop=(k == 1)),
                      ["wukvb", "ckvn"], [PK[b]])
            if g % 2 == 0:
                V(lambda e, g=g, b=b: e.tensor_copy(out=Vt[:, g * 4:(g + 1) * 4, :], in_=ps[b][:].rearrange("p (u d) -> p u d", u=4)), [PK[b]], ["Vt"])
            else:
                A(lambda e, g=g, b=b: e.activation(out=Vt[:, g * 4:(g + 1) * 4, :], in_=ps[b][:].rearrange("p (u d) -> p u d", u=4), func=AF.Copy), [PK[b]], ["Vt"])
        if h == 0:
            qproj(0, 0)

        for j in range(8):
            nkb = 8 * j + 8
            fi = fin_i[0] % 2
            fin_i[0] += 1
            ob = 4 + 2 * fi
            lb = 5 + 2 * fi
            DQ(lambda e, j=j, fi=fi, h=h: e.dma_start(out=thbt[fi][:], in_=thb_d[:, h, j * 512:(j + 1) * 512]), ["thb_d"], ["thbt%d" % fi], "thbt%d" % fi)
            DQ(lambda e, j=j, fi=fi, h=h: e.dma_start(out=mat[fi][:], in_=ma_d[:, h, j * 512:(j + 1) * 512]), ["ma_d"], ["mat%d" % fi], "mat%d" % fi)
            q0 = j * 512

            def qk(kb):
                sbk = kb % 3
                kl = kb - (8 * j + 4)
                c0 = 128 * kl if kl > 0 else 0
                T(lambda e, kb=kb, sbk=sbk, c0=c0, q0=q0: e.matmul(ps[sbk][:, c0:512], lhsT=KT[:, kb * 128:(kb + 1) * 128], rhs=QT[:, q0 + c0:q0 + 512],
                                                            start=True, stop=False), ["KT", "QT%d" % j], [PK[sbk]])
                T(lambda e, kb=kb, sbk=sbk, c0=c0, kl=kl, q0=q0: e.matmul(ps[sbk][:, c0:512], lhsT=krope[:, kb * 128:(kb + 1) * 128], rhs=QR[:, q0 + c0:q0 + 512],
                                                                   start=False, stop=(kl < 0)), ["krope", "QR%d" % j, "kropez", "QRz"], [PK[sbk]])
                if kl >= 0:
                    T(lambda e, sbk=sbk, c0=c0: e.matmul(ps[sbk][:, c0:c0 + 128], lhsT=ident[:], rhs=cmask[:], start=False, stop=True),
                      ["ident", "cmask"], [PK[sbk]])

            def ex(kb):
                sbk = kb % 3
                pi = kb % NPT
                kl = kb - (8 * j + 4)
                c0 = 128 * kl if kl > 0 else 0
                A(lambda e, sbk=sbk, pi=pi, c0=c0: e.activation(out=PT[pi][:, c0:512], in_=ps[sbk][:, c0:512], func=AF.Exp, scale=SCALE),
                  [PK[sbk]], ["PT%d" % pi])

            def pv(kb):
                pi = kb % NPT
                kl = kb - (8 * j + 4)
                c0 = 128 * kl if kl > 0 else 0
                T(lambda e, kb=kb, pi=pi, c0=c0, ob=ob, nkb=nkb: e.matmul(ps[ob][:, c0:512], lhsT=Vt[:, kb, :], rhs=PT[pi][:, c0:512], start=(kb == 0), stop=(kb == nkb - 1)),
                  ["Vt", "PT%d" % pi], [PK[ob]])
                m6 = kb % 12
                if m6 in (0, 4, 8):
                    T(lambda e, kb=kb, pi=pi, c0=c0, lb=lb: e.matmul(ps[lb][:, c0:512], lhsT=ones[:], rhs=PT[pi][:, c0:512], start=(kb == 0), stop=False),
                      ["ones", "PT%d" % pi], [PK[lb]])
                elif m6 in (2, 10):
                    G(lambda e, pi=pi, c0=c0: e.tensor_tensor(out=accP[:, c0:512], in0=PT[pi][:, c0:512], in1=accP[:, c0:512], op=ALU.add),
                      ["PT%d" % pi, "accP"], ["accP"])
                elif m6 in (1, 5, 7, 11):
                    V(lambda e, pi=pi, c0=c0: e.tensor_tensor(out=accD[:, c0:512], in0=PT[pi][:, c0:512], in1=accD[:, c0:512], op=ALU.add),
                      ["PT%d" % pi, "accD"], ["accD"])
                else:
                    V(lambda e, pi=pi, c0=c0: e.tensor_tensor(out=accE[:, c0:512], in0=PT[pi][:, c0:512], in1=accE[:, c0:512], op=ALU.add),
                      ["PT%d" % pi, "accE"], ["accE"])

            V(lambda e: e.memset(accD[:], 0.0), [], ["accD"])
            V(lambda e: e.memset(accE[:], 0.0), [], ["accE"])
            G(lambda e: e.memset(accP[:], 0.0), [], ["accP"])
            LA = 3
            for kk in range(LA):
                qk(kk)
                ex(kk)
            for kb in range(nkb):
                if kb + LA < nkb:
                    qk(kb + LA)
                    ex(kb + LA)
                pv(kb)
                if kb == 4:
                    if j + 1 < 8:
                        qproj(h, j + 1)
                    elif h + 1 < 8:
                        qproj(h + 1, 0)
            V(lambda e: e.tensor_tensor(out=accD[:], in0=accD[:], in1=accP[:], op=ALU.add), ["accD", "accP"], ["accD"])
            V(lambda e: e.tensor_tensor(out=accD[:], in0=accD[:], in1=accE[:], op=ALU.add), ["accD", "accE"], ["accD"])
            T(lambda e, lb=lb: e.matmul(ps[lb][:], lhsT=onesf[:], rhs=accD[:], start=False, stop=True), ["onesf", "accD"], [PK[lb]])
            V(lambda e, lb=lb: e.reciprocal(out=rcp[:], in_=ps[lb][:]), [PK[lb]], ["rcp"])
            V(lambda e, ob=ob: e.tensor_tensor(out=ot[:], in0=ps[ob][:], in1=rcp[:], op=ALU.mult), [PK[ob], "rcp"], ["ot"])
            V(lambda e, fi=fi: e.scalar_tensor_tensor(out=ot[:], in0=thbt[fi][:], scalar=1.0, in1=ot[:], op0=ALU.add, op1=ALU.mult),
              ["thbt%d" % fi, "ot"], ["ot"])
            V(lambda e, fi=fi: e.scalar_tensor_tensor(out=mgt[fi][:], in0=ot[:], scalar=0.5, in1=mat[fi][:], op0=ALU.mult, op1=ALU.add),
              ["ot", "mat%d" % fi], ["mgt%d" % fi])
            GQ(lambda e, j=j, fi=fi, h=h: e.dma_start(out=mg_d[:, h, j * 512:(j + 1) * 512], in_=mgt[fi][:]), ["mgt%d" % fi], ["mg_d"], "mgt%d" % fi)

    S.barrier()

    sb.top = persist_top
    wupb = sb.alloc("wupb", [128, 8, 4096], BF16)
    wdnb = sb.alloc("wdnb", [128, 32, D], BF16)
    wdn_top = sb.top
    assert sb.top <= wout_top - 8 * D * 2, (sb.top, wout_top)
    sb.top = wout_top
    mgin = [sb.alloc("mgin%d" % i, [128, 8, 512], BF16) for i in range(2)]
    xo = [sb.alloc("xo%d" % i, [128, D], F32) for i in range(2)]
    h1t = [sb.alloc("h1t%d" % i, [128, D], F32) for i in range(2)]
    h1n = [sb.alloc("h1n%d" % i, [128, D], BF16) for i in range(2)]
    h1nTs = [sb.alloc("h1nTs%d" % i, [128, 8, 128], BF16) for i in range(2)]
    ss3 = [sb.alloc("ss3_%d" % i, [128, 1], F32) for i in range(2)]
    rs3 = [sb.alloc("rs3_%d" % i, [128, 1], F32) for i in range(2)]
    phase3a_peak = sb.top

    wup_src = w_up.rearrange("(k p) n -> p k n", p=128)
    for k in range(8):
        for c0 in range(0, 4096, 1024):
            GQ(lambda e, k=k, c0=c0: e.dma_start(out=wupb[:, k, c0:c0 + 1024], in_=wup_src[:, k, c0:c0 + 1024]), [],
               ["wupb" if (k == 7 and c0 == 3072) else "wupb_p%d_%d" % (k, c0)], "wupb")
    wdn_src = w_down.rearrange("(k p) n -> p k n", p=128)
    for k in range(32):
        GQ(lambda e, k=k: e.dma_start(out=wdnb[:, k, :], in_=wdn_src[:, k, :]), [], ["wdnb" if k == 31 else "wdnb_p%d" % k], "wdnb")

    for tt in range(8):
        mi = tt % 2
        DQ(lambda e, tt=tt, mi=mi: e.dma_start(out=mgin[mi][:], in_=mg_d[:, :, tt * 512:(tt + 1) * 512]), ["mg_d"], ["mgin%d" % mi], "mgin%d" % mi)
        for s in range(4):
            i = s % 2
            r0 = tt * 512 + s * 128
            DQ(lambda e, i=i, tt=tt, s=s: e.dma_start(out=xo[i][:], in_=x_all[(2 * tt + 1) * 512 + s * 128:(2 * tt + 1) * 512 + s * 128 + 128, :]), [], ["xo%d" % i], "xo%d" % i)
            for hf in range(2):
                b = 2 + hf + 2 * i
                for k in range(8):
                    T(lambda e, k=k, hf=hf, b=b, s=s, mi=mi: e.matmul(ps[b][:], lhsT=mgin[mi][:, k, s * 128:(s + 1) * 128], rhs=woutb[:, k, hf * 512:(hf + 1) * 512],
                                                                      start=(k == 0), stop=(k == 7)), ["mgin%d" % mi, "woutb"], [PK[b]])
                V(lambda e, hf=hf, b=b, i=i: e.tensor_tensor(out=h1t[i][:, hf * 512:(hf + 1) * 512], in0=ps[b][:], in1=xo[i][:, hf * 512:(hf + 1) * 512], op=ALU.add),
                  [PK[b], "xo%d" % i], ["h1t%d" % i])
            GQ(lambda e, i=i, r0=r0: e.dma_start(out=h1_d[r0:r0 + 128, :], in_=h1t[i][:]), ["h1t%d" % i], ["h1_d"], "h1t%d" % i)
            A(lambda e, i=i: e.activation(out=xo[i][:], in_=h1t[i][:], func=AF.Square, accum_out=ss3[i][:]), ["h1t%d" % i], ["xo%d" % i, "ss3_%d" % i], small=True)
            A(lambda e, i=i: e.activation(out=rs3[i][:], in_=ss3[i][:], func=AF.Sqrt, scale=1.0 / D, bias=EPS), ["ss3_%d" % i], ["rs3_%d" % i], small=True)
            V(lambda e, i=i: e.reciprocal(out=rs3[i][:], in_=rs3[i][:]), ["rs3_%d" % i], ["rs3_%d" % i], small=True)
            A(lambda e, i=i: e.activation(out=h1n[i][:], in_=h1t[i][:], func=AF.Copy, scale=rs3[i][:, 0:1]), ["h1t%d" % i, "rs3_%d" % i], ["h1n%d" % i])
            pT = ps[i][:].bitcast(BF16)
            for c in range(8):
                T(lambda e, i=i, c=c, pT=pT: e.transpose(out=pT[:, c * 128:(c + 1) * 128], in_=h1n[i][:, c * 128:(c + 1) * 128], identity=ident[:]),
                  ["h1n%d" % i, "ident"], [PK[i]])
            V(lambda e, i=i, pT=pT: e.tensor_tensor(out=h1nTs[i][:], in0=pT.rearrange("p (c t) -> p c t", c=8), in1=gB2[:], op=ALU.mult),
              [PK[i], "gB2"], ["h1nTs%d" % i])
            GQ(lambda e, i=i, r0=r0: e.dma_start(out=h1nT_d[:, :, r0:r0 + 128], in_=h1nTs[i][:]), ["h1nTs%d" % i], ["h1nT_d"], "h1nTs%d" % i)

    S.barrier()

    sb.top = wdn_top
    hin = [sb.alloc("hin%d" % i, [128, 8, 512], BF16) for i in range(2)]
    actT = sb.alloc("actT", [128, 32, 512], BF16)
    rl = [sb.alloc("rl%d" % i, [128, 512], F32) for i in range(2)]
    h1r = [sb.alloc("h1r%d" % i, [128, D], F32) for i in range(2)]
    op_ = [sb.alloc("op%d" % i, [128, D], F32) for i in range(2)]
    nfb = sb.alloc("nfb", [128, D], F32)
    ss4 = [sb.alloc("ss4_%d" % i, [128, 1], F32) for i in range(2)]
    rs4 = [sb.alloc("rs4_%d" % i, [128, 1], F32) for i in range(2)]
    phase3b_peak = sb.top

    DQ(lambda e: e.dma_start(out=nfb[:], in_=nfb_d), [], ["nfb"], "nfb")
    urot = [0]
    for tt in range(8):
        hi = tt % 2
        DQ(lambda e, tt=tt, hi=hi: e.dma_start(out=hin[hi][:], in_=h1nT_d[:, :, tt * 512:(tt + 1) * 512]), ["h1nT_d"], ["hin%d" % hi], "hin%d" % hi)
        for f in range(32):
            b = urot[0] % 4
            urot[0] += 1
            ri = f % 2
            for k in range(8):
                T(lambda e, k=k, f=f, b=b, hi=hi: e.matmul(ps[b][:], lhsT=wupb[:, k, f * 128:(f + 1) * 128], rhs=hin[hi][:, k, :], start=(k == 0), stop=(k == 7)),
                  ["wupb", "hin%d" % hi], [PK[b]])
            A(lambda e, b=b, ri=ri: e.activation(out=rl[ri][:], in_=ps[b][:], func=AF.Relu), [PK[b]], ["rl%d" % ri])
            V(lambda e, b=b, ri=ri, f=f: e.tensor_tensor(out=actT[:, f, :], in0=ps[b][:], in1=rl[ri][:], op=ALU.mult), [PK[b], "rl%d" % ri], ["actT%d" % f])
        for s in range(4):
            i = s % 2
            r0 = tt * 512 + s * 128
            DQ(lambda e, i=i, r0=r0: e.dma_start(out=h1r[i][:], in_=h1_d[r0:r0 + 128, :]), ["h1_d"], ["h1r%d" % i], "h1r%d" % i)
            for hf in range(2):
                b = 4 + hf + 2 * i
                for f in range(32):
                    T(lambda e, f=f, hf=hf, b=b, s=s: e.matmul(ps[b][:], lhsT=actT[:, f, s * 128:(s + 1) * 128], rhs=wdnb[:, f, hf * 512:(hf + 1) * 512],
                                                               start=(f == 0), stop=(f == 31)), ["actT%d" % f, "wdnb"], [PK[b]])
                V(lambda e, hf=hf, b=b, i=i: e.tensor_tensor(out=op_[i][:, hf * 512:(hf + 1) * 512], in0=ps[b][:], in1=h1r[i][:, hf * 512:(hf + 1) * 512], op=ALU.add),
                  [PK[b], "h1r%d" % i], ["op%d" % i])
            A(lambda e, i=i: e.activation(out=h1r[i][:], in_=op_[i][:], func=AF.Square, accum_out=ss4[i][:]), ["op%d" % i], ["h1r%d" % i, "ss4_%d" % i], small=True)
            A(lambda e, i=i: e.activation(out=rs4[i][:], in_=ss4[i][:], func=AF.Sqrt, scale=1.0 / D, bias=EPS), ["ss4_%d" % i], ["rs4_%d" % i], small=True)
            V(lambda e, i=i: e.reciprocal(out=rs4[i][:], in_=rs4[i][:]), ["rs4_%d" % i], ["rs4_%d" % i], small=True)
            V(lambda e, i=i: e.scalar_tensor_tensor(out=op_[i][:], in0=op_[i][:], scalar=rs4[i][:, 0:1], in1=nfb[:], op0=ALU.mult, op1=ALU.mult),
              ["op%d" % i, "rs4_%d" % i, "nfb"], ["op%d" % i])
            GQ(lambda e, i=i, r0=r0: e.dma_start(out=out[r0:r0 + 128, :], in_=op_[i][:]), ["op%d" % i], ["out"], "op%d" % i)

    S.barrier()
    S.op("sync", lambda e: e.nop())

    S.finalize(nc, st)
    with nc.Block() as block:
        S.emit(block)
    st.close()
    nc._peaks = (phase1_peak, phase2_peak, phase3a_peak, phase3b_peak)
    return nc


def _host_prep(inp):
    f32 = np.float32
    x = np.asarray(inp["x"], f32)
    w_in = np.asarray(inp["w_in"], f32)
    perm = (np.arange(64) + 32) % 64
    o_rx, o_rg, o_cq, o_ckv, o_kr, o_ga, o_gb = 0, 1024, 2048, 2304, 2560, 2624, 3648
    kr = w_in[:, o_kr:o_kr + 64]
    w_in_r = np.concatenate([w_in[:, o_rx:o_rx + 1024], w_in[:, o_ckv:o_ckv + 256], kr, kr[:, perm],
                             w_in[:, o_rg:o_rg + 1024], w_in[:, o_ga:o_ga + 1024], w_in[:, o_gb:o_gb + 1024],
                             w_in[:, o_cq:o_cq + 256]], axis=1)
    w_in_r = np.ascontiguousarray(w_in_r)
    assert w_in_r.shape[1] == WIN_COLS
    w_uq = np.asarray(inp["w_uq"], f32).reshape(256, 8, 192)
    w_uq_r = np.concatenate([w_uq[:, :, :128], w_uq[:, :, 128:], w_uq[:, :, 128:][:, :, perm]], axis=2).reshape(256, 2048)
    w_uq_r = np.ascontiguousarray(w_uq_r)

    def colv(v):
        v = np.asarray(v, f32).reshape(-1, 128)
        return v.T

    vecs_common = np.zeros((128, NV), f32)
    vecs_common[:, VC["nmix"]:VC["nmix"] + 8] = colv(inp["norm_mix"])
    cw = np.asarray(inp["conv_w"], f32)
    for k in range(4):
        vecs_common[:, VC["cw"] + 8 * k:VC["cw"] + 8 * k + 8] = colv(cw[k])
    vecs_common[:, VC["cb"]:VC["cb"] + 8] = colv(inp["conv_b"])
    vecs_common[:, VC["ba"]:VC["ba"] + 8] = colv(np.asarray(inp["lru_ba"]).reshape(-1))
    vecs_common[:, VC["bx"]:VC["bx"] + 8] = colv(np.asarray(inp["lru_bx"]).reshape(-1))
    vecs_common[:, VC["lam"]:VC["lam"] + 8] = colv(inp["lru_lambda"])
    vecs_common[:, VC["qn"]:VC["qn"] + 2] = colv(inp["q_norm"])
    vecs_common[:, VC["kvn"]:VC["kvn"] + 2] = colv(inp["kv_norm"])
    vecs_common[:, VC["nmlp"]:VC["nmlp"] + 8] = colv(inp["norm_mlp"])
    nfb = np.ascontiguousarray(np.broadcast_to(np.asarray(inp["norm_final"], f32)[None, :], (128, D)))

    inv_freq = 1.0 / (10000.0 ** (np.arange(0, 64, 2, dtype=np.float64) / 64.0))
    inv_freq = inv_freq.astype(f32).astype(np.float64)

    def tables(pos):
        ang = (pos.astype(f32)[:, None] * inv_freq.astype(f32)[None, :]).astype(np.float64)
        cos = np.cos(ang)
        sin = np.sin(ang)
        cosT = np.concatenate([cos, cos], -1).T
        sinT = np.concatenate([-sin, sin], -1).T
        return np.ascontiguousarray(cosT.astype(f32)), np.ascontiguousarray(sinT.astype(f32))

    ident = np.eye(128, dtype=f32)
    kk = np.arange(128)[:, None]
    qq = np.arange(128)[None, :]
    cmask = np.where(qq >= kk, 0.0, -30000.0).astype(f32)

    shared = {
        "w_in": w_in_r, "w_uq": w_uq_r, "w_ukv": np.ascontiguousarray(np.asarray(inp["w_ukv"], f32)),
        "w_out": np.ascontiguousarray(np.asarray(inp["w_out"], f32)), "w_up": np.ascontiguousarray(np.asarray(inp["w_up"], f32)),
        "w_down": np.ascontiguousarray(np.asarray(inp["w_down"], f32)),
        "lru_wa": np.ascontiguousarray(np.asarray(inp["lru_wa"], f32)), "lru_wx": np.ascontiguousarray(np.asarray(inp["lru_wx"], f32)),
        "nfb": nfb, "ident": ident, "cmask": cmask,
    }
    in_maps = []
    for core in range(8):
        b, half = core // 2, core % 2
        if half == 1:
            x_all = np.ascontiguousarray(x[b])
            pos = np.arange(SEQ)
            flag, kbias = 1.0, 0.0
        else:
            x_all = np.concatenate([np.zeros((512, D), f32), x[b, :SEQ - 512]], axis=0)
            pos = np.concatenate([np.zeros(512), np.arange(SEQ - 512)])
            flag, kbias = 0.0, -150.0
        cosT, sinT = tables(pos)
        vecs = vecs_common.copy()
        vecs[:, VC["flag"]] = flag
        vecs[:, VC["kbias"]] = kbias
        m = dict(shared)
        kbrow = np.zeros((1, SEQ), f32)
        if half == 0:
            kbrow[0, :512] = kbias / (192 ** -0.5)
        m.update({"x_all": x_all, "vecs": vecs, "cosT": cosT, "sinT": sinT, "kbrow": kbrow})
        in_maps.append(m)
    return in_maps


_NC_CACHE = {}


def kernel(**inputs):
    in_maps = _host_prep(inputs)
    if "nc" not in _NC_CACHE:
        _NC_CACHE["nc"] = build_program()
    nc = _NC_CACHE["nc"]
    res = run_bass_kernel_spmd(nc, in_maps, core_ids=list(range(8)))
    out = np.empty((4, SEQ, D), np.float32)
    for core in range(8):
        b, half = core // 2, core % 2
        ro = res.results[core]["out"]
        for i in range(8):
            out[b, (2 * i + half) * 512:(2 * i + half + 1) * 512] = ro[i * 512:(i + 1) * 512]
    return out
```
